# Optimizing a Trainium2 kernel written in Bass

```python
import math
import jax
import jax.numpy as jnp
from jax import lax
import numpy as np

D_MODEL = 2048
BATCH = 8
SEQ = 4096
DEPTH = 4

CHUNK = 64
Q_BLOCK = 128
NORM_EPS = 1e-6
ROPE_THETA = 10000.0

FOX_HEADS = 6
FOX_DH = 128
FOX_W = FOX_HEADS * FOX_DH
FORGET_BIAS_CENTER = 3.0

MLA_HEADS = 6
MLA_NOPE = 128
MLA_ROPE = 64
MLA_V = 128
MLA_Q_LORA = 512
MLA_KV_LORA = 256
MLA_W = MLA_HEADS * MLA_V

RET_HEADS = 4
RET_DK = 128
RET_DV = 256
RET_QK_W = RET_HEADS * RET_DK
RET_V_W = RET_HEADS * RET_DV

N_BRANCH = 3

D_FF = 5632
CONV_W = 3

IN_SPLITS = (FOX_W, FOX_W, FOX_W, FOX_HEADS,
             MLA_Q_LORA, MLA_KV_LORA, MLA_ROPE,
             RET_QK_W, RET_QK_W, RET_V_W, RET_V_W,
             N_BRANCH * D_MODEL)
IN_WIDTH = sum(IN_SPLITS)

kernel_name = 'hybrid_fox_mla_retention_convffn'


def rms_norm(x, g):
    xf = x.astype(jnp.float32)
    y = xf * lax.rsqrt(jnp.mean(xf * xf, axis=-1, keepdims=True) + NORM_EPS)
    return (y * g.astype(jnp.float32)).astype(x.dtype)


def apply_rope(x):
    s, d = x.shape[1], x.shape[-1]
    pos = jnp.arange(s, dtype=jnp.float32)
    inv_freq = ROPE_THETA ** (-jnp.arange(0, d, 2, dtype=jnp.float32) / d)
    ang = pos[:, None] * inv_freq[None, :]
    cos = jnp.cos(ang)[None, :, None, :]
    sin = jnp.sin(ang)[None, :, None, :]
    xf = x.astype(jnp.float32)
    x1, x2 = xf[..., : d // 2], xf[..., d // 2:]
    return jnp.concatenate([x1 * cos - x2 * sin, x2 * cos + x1 * sin], axis=-1).astype(x.dtype)


def block_attention(q, k, v, scale, frame_causal, log_decay_cum):
    s_len = q.shape[2]
    outs = []
    for s0 in range(0, s_len, Q_BLOCK):
        s1 = s0 + Q_BLOCK
        logits = jnp.einsum('bhqd,bhkd->bhqk', q[:, :, s0:s1], k[:, :, :s1]).astype(jnp.float32) * scale
        if log_decay_cum is not None:
            logits = logits + log_decay_cum[:, :, s0:s1, None] - log_decay_cum[:, :, None, :s1]
        q_pos = jnp.arange(s0, s1)
        k_pos = jnp.arange(s1)
        if frame_causal:
            mask = k_pos[None, :] <= q_pos[:, None]
        else:
            mask = (k_pos // CHUNK)[None, :] <= (q_pos // CHUNK)[:, None]
        p = jax.nn.softmax(jnp.where(mask, logits, -jnp.inf), axis=-1).astype(v.dtype)
        outs.append(jnp.einsum('bhqk,bhkd->bhqd', p, v[:, :, :s1]))
    return jnp.concatenate(outs, axis=2)


def fox_mixer(q, k, v, f_logit, b_f):
    b, s = q.shape[:2]
    def heads(t):
        return t.reshape(b, s, FOX_HEADS, FOX_DH).transpose(0, 2, 1, 3)
    log_f = jax.nn.log_sigmoid(f_logit.astype(jnp.float32) + b_f.astype(jnp.float32))
    c = jnp.cumsum(log_f, axis=1).transpose(0, 2, 1)
    o = block_attention(heads(q), heads(k), heads(v), FOX_DH ** -0.5, True, c)
    return o.transpose(0, 2, 1, 3).reshape(b, s, FOX_W)


def mla_mixer(c_q, c_kv, k_rope, q_norm_g, kv_norm_g, w_uq, w_ukv):
    b, s = c_q.shape[:2]
    q = (rms_norm(c_q, q_norm_g) @ w_uq).reshape(b, s, MLA_HEADS, MLA_NOPE + MLA_ROPE)
    q = jnp.concatenate([q[..., :MLA_NOPE], apply_rope(q[..., MLA_NOPE:])], axis=-1)
    kv = (rms_norm(c_kv, kv_norm_g) @ w_ukv).reshape(b, s, MLA_HEADS, MLA_NOPE + MLA_V)
    k_nope, v = kv[..., :MLA_NOPE], kv[..., MLA_NOPE:]
    k_r = apply_rope(k_rope[:, :, None, :])
    k = jnp.concatenate([k_nope, jnp.broadcast_to(k_r, (b, s, MLA_HEADS, MLA_ROPE))], axis=-1)
    o = block_attention(q.transpose(0, 2, 1, 3), k.transpose(0, 2, 1, 3), v.transpose(0, 2, 1, 3),
                        (MLA_NOPE + MLA_ROPE) ** -0.5, False, None)
    return o.transpose(0, 2, 1, 3).reshape(b, s, MLA_W)


def retention_mixer(q, k, v, g):
    b, s = q.shape[:2]
    n = s // CHUNK
    dt = v.dtype
    q = apply_rope(q.reshape(b, s, RET_HEADS, RET_DK))
    k = apply_rope(k.reshape(b, s, RET_HEADS, RET_DK)) * (RET_DK ** -0.5)
    qc = q.reshape(b, n, CHUNK, RET_HEADS, RET_DK)
    kc = k.reshape(b, n, CHUNK, RET_HEADS, RET_DK)
    vc = v.reshape(b, n, CHUNK, RET_HEADS, RET_DV)
    log_gamma = jnp.log(1.0 - 2.0 ** (-5.0 - jnp.arange(RET_HEADS, dtype=jnp.float32)))
    idx = jnp.arange(CHUNK, dtype=jnp.float32)
    intra_decay = jnp.exp(log_gamma[:, None, None] * jnp.abs(idx[:, None] - idx[None, :]))
    state_in = jnp.exp(log_gamma[:, None] * (CHUNK - 1 - idx)[None, :])
    cross_decay = jnp.exp(log_gamma[:, None] * (idx + 1.0)[None, :])
    chunk_decay = jnp.exp(log_gamma * CHUNK)
    scores = jnp.einsum('bnjhd,bnlhd->bnhjl', qc, kc) * intra_decay.astype(dt)
    intra = jnp.einsum('bnhjl,bnlhe->bnjhe', scores, vc)
    kv_chunk = jnp.einsum('bnlhd,hl,bnlhe->nbhde', kc, state_in.astype(dt), vc).astype(jnp.float32)
    def step(state, kv_n):
        return chunk_decay[None, :, None, None] * state + kv_n, state
    _, prev = lax.scan(step, jnp.zeros((b, RET_HEADS, RET_DK, RET_DV), jnp.float32), kv_chunk)
    cross = jnp.einsum('bnjhd,nbhde->bnjhe', qc, prev.astype(dt)) * cross_decay.T[None, None, :, :, None].astype(dt)
    o = (intra + cross).reshape(b, s, RET_HEADS, RET_DV).astype(jnp.float32)
    o = o * lax.rsqrt(jnp.mean(o * o, axis=-1, keepdims=True) + NORM_EPS)
    return o.reshape(b, s, RET_V_W).astype(dt) * jax.nn.silu(g)


def conv_ffn(h, w_up, w_gate, conv_w, conv_b, w_down):
    s = h.shape[1]
    u = h @ w_up
    u_pad = jnp.pad(u, ((0, 0), (CONV_W - 1, 0), (0, 0)))
    u_conv = conv_b + sum(conv_w[i] * u_pad[:, i:i + s] for i in range(CONV_W))
    return (jax.nn.gelu(u_conv) * (h @ w_gate)) @ w_down


def _normal(key, shape, fan_in):
    return jax.random.normal(key, shape, jnp.float32) * (fan_in ** -0.5)


def _gain(key, shape):
    return 1.0 + 0.02 * jax.random.normal(key, shape, jnp.float32)


def setup_inputs(seed: int = 0) -> dict:
    key = jax.random.key(seed)
    ks = jax.random.split(key, 20)
    L, D = DEPTH, D_MODEL
    return {
        'x': jax.random.normal(ks[0], (BATCH, SEQ, D), jnp.float32),
        'norm1_g': _gain(ks[1], (L, D)),
        'w_in': _normal(ks[2], (L, D, IN_WIDTH), D),
        'mla_q_norm_g': _gain(ks[3], (L, MLA_Q_LORA)),
        'mla_kv_norm_g': _gain(ks[4], (L, MLA_KV_LORA)),
        'mla_w_uq': _normal(ks[5], (L, MLA_Q_LORA, MLA_HEADS * (MLA_NOPE + MLA_ROPE)), MLA_Q_LORA),
        'mla_w_ukv': _normal(ks[6], (L, MLA_KV_LORA, MLA_HEADS * (MLA_NOPE + MLA_V)), MLA_KV_LORA),
        'fox_b_f': FORGET_BIAS_CENTER + 0.1 * jax.random.normal(ks[7], (L, FOX_HEADS), jnp.float32),
        'w_br_fox': _normal(ks[8], (L, FOX_W, D), FOX_W),
        'w_br_mla': _normal(ks[9], (L, MLA_W, D), MLA_W),
        'w_br_ret': _normal(ks[10], (L, RET_V_W, D), RET_V_W),
        'w_out': _normal(ks[11], (L, D, D), D),
        'norm2_g': _gain(ks[12], (L, D)),
        'ffn_w_up': _normal(ks[13], (L, D, D_FF), D),
        'ffn_w_gate': _normal(ks[14], (L, D, D_FF), D),
        'ffn_conv_w': _normal(ks[15], (L, CONV_W, D_FF), CONV_W),
        'ffn_conv_b': 0.02 * jax.random.normal(ks[16], (L, D_FF), jnp.float32),
        'ffn_w_down': _normal(ks[17], (L, D_FF, D), D_FF),
        'final_norm_g': _gain(ks[18], (D,)),
    }


def reference(x, norm1_g, w_in, mla_q_norm_g, mla_kv_norm_g, mla_w_uq, mla_w_ukv, fox_b_f,
              w_br_fox, w_br_mla, w_br_ret, w_out, norm2_g, ffn_w_up, ffn_w_gate,
              ffn_conv_w, ffn_conv_b, ffn_w_down, final_norm_g):
    b, s, d = x.shape
    split_points = [int(p) for p in np.cumsum(IN_SPLITS)[:-1]]
    for i in range(DEPTH):
        h = rms_norm(x, norm1_g[i])
        (fq, fk, fv, ff, mq, mkv, mkr, rq, rk, rv, rg, gates) = jnp.split(h @ w_in[i], split_points, axis=-1)
        a = fox_mixer(fq, fk, fv, ff, fox_b_f[i])
        bm = mla_mixer(mq, mkv, mkr, mla_q_norm_g[i], mla_kv_norm_g[i], mla_w_uq[i], mla_w_ukv[i])
        c = retention_mixer(rq, rk, rv, rg)
        g = jax.nn.sigmoid(gates.astype(jnp.float32)).astype(x.dtype).reshape(b, s, N_BRANCH, d)
        merged = (g[:, :, 0] * (a @ w_br_fox[i])
                  + g[:, :, 1] * (bm @ w_br_mla[i])
                  + g[:, :, 2] * (c @ w_br_ret[i]))
        x = x + merged @ w_out[i]
        x = x + conv_ffn(rms_norm(x, norm2_g[i]), ffn_w_up[i], ffn_w_gate[i],
                         ffn_conv_w[i], ffn_conv_b[i], ffn_w_down[i])
    return rms_norm(x, final_norm_g)
```

```python
import math
import contextlib
import numpy as np
import concourse.bass as bass
import concourse.mybir as mybir
from concourse.bass_utils import run_bass_kernel_spmd

F32 = mybir.dt.float32
BF16 = mybir.dt.bfloat16
I32 = mybir.dt.int32
AF = mybir.ActivationFunctionType
ALU = mybir.AluOpType

ENGS = ("pe", "act", "dve", "pool", "sp")
NDMASEM = 24

D = 2048
KC = 16
DEPTH = 4
SEQ = 4096
BATCH = 8
DFF = 5632
NFC = 44
EPS = 1e-6
O_FQ, O_FK, O_FV, O_FF, O_MQ, O_MKV, O_MKR, O_RQ, O_RK, O_RV, O_RG, O_GT = (
    0, 768, 1536, 2304, 2310, 2822, 3078, 3142, 3654, 4166, 5190, 6214)
INW = 12358


class Buf:
    __slots__ = ("name", "w", "r", "big")

    def __init__(self, name="", big=False):
        self.name = name
        self.w = None
        self.r = []
        self.big = big


class Op:
    __slots__ = ("eng", "fn", "deps", "dma", "sig", "sem", "val", "qi")

    def __init__(self, eng, fn, dma):
        self.eng = eng
        self.fn = fn
        self.dma = dma
        self.deps = ()
        self.sig = False
        self.sem = None
        self.val = 0
        self.qi = 0


class Sched:
    def __init__(self, nc):
        self.nc = nc
        self.ops = []
        self.per_eng = {e: [] for e in ENGS}
        self.ndma = {e: 0 for e in ENGS}
        self.bigtok = Buf("bigtok")

    def add(self, eng, fn, reads=(), writes=(), dma=False):
        idx = len(self.ops)
        op = Op(eng, fn, dma)
        reads = list(reads)
        for b in list(reads) + list(writes):
            if b.big:
                reads.append(self.bigtok)
                break
        deps = set()
        for b in reads:
            if b.w is not None:
                deps.add(b.w)
        for b in writes:
            if b.w is not None:
                deps.add(b.w)
            deps.update(b.r)
        for b in reads:
            if not dma:
                b.r = [j for j in b.r if self.ops[j].dma or self.ops[j].eng != eng]
            b.r.append(idx)
        for b in writes:
            b.w = idx
            b.r = []
        deps.discard(idx)
        fd = []
        for d in deps:
            o = self.ops[d]
            if eng == "pe" and o.eng == "pe" and not dma and not o.dma:
                continue
            fd.append(d)
            o.sig = True
        op.deps = fd
        if dma:
            op.sig = True
            op.qi = self.ndma[eng]
            self.ndma[eng] += 1
        self.ops.append(op)
        self.per_eng[eng].append(idx)
        return idx

    def dma(self, eng, out, in_, reads=(), writes=(), slow=False):
        if eng == "sp" and type(out.tensor).__name__ == "DRamTensorHandle":
            eng = "act"
        if slow:
            return self.add(eng, lambda e: e.dma_start(out=out, in_=in_, allow_slow_non_contiguous=True),
                            reads, writes, dma=True)
        return self.add(eng, lambda e: e.dma_start(out=out, in_=in_), reads, writes, dma=True)

    def emit(self, final_wait_eng="sp"):
        nc = self.nc
        ops = self.ops
        with contextlib.ExitStack() as st:
            csem = {e: st.enter_context(nc.semaphore("c_" + e)) for e in ENGS}
            dsem = {e: [st.enter_context(nc.semaphore("d_%s_%d" % (e, i))) for i in range(NDMASEM)]
                    for e in ENGS if self.ndma[e] > 0}
            cnt = {e: 0 for e in ENGS}
            for op in ops:
                if op.dma:
                    op.sem = dsem[op.eng][op.qi % NDMASEM]
                    op.val = 16 * (op.qi // NDMASEM + 1)
                elif op.sig:
                    cnt[op.eng] += 1
                    op.sem = csem[op.eng]
                    op.val = cnt[op.eng]
            self.maxval = dict(cnt)
            block = st.enter_context(nc.Block())

            def make(ename):
                idxs = self.per_eng[ename]

                def body(eng):
                    waited = {}
                    for i in idxs:
                        op = ops[i]
                        need = {}
                        for d in op.deps:
                            o = ops[d]
                            k = id(o.sem)
                            if k not in need or need[k][1] < o.val:
                                need[k] = (o.sem, o.val)
                        if op.dma and op.qi >= NDMASEM:
                            s = dsem[ename][op.qi % NDMASEM]
                            v = 16 * (op.qi // NDMASEM)
                            k = id(s)
                            if k not in need or need[k][1] < v:
                                need[k] = (s, v)
                        for k, (s, v) in need.items():
                            if waited.get(k, 0) >= v:
                                continue
                            eng.wait_ge(s, v)
                            waited[k] = v
                        ins = op.fn(eng)
                        if op.sig:
                            ins.then_inc(op.sem, 16 if op.dma else 1)
                    if ename == final_wait_eng:
                        for e2 in dsem:
                            n = self.ndma[e2]
                            for j in range(min(n, NDMASEM)):
                                last_q = j + ((n - 1 - j) // NDMASEM) * NDMASEM
                                eng.wait_ge(dsem[e2][j], 16 * (last_q // NDMASEM + 1))
                return body

            if self.per_eng["pe"]:
                block.tensor(make("pe"))
            if self.per_eng["act"]:
                block.scalar(make("act"))
            if self.per_eng["dve"]:
                block.vector(make("dve"))
            if self.per_eng["pool"]:
                block.gpsimd(make("pool"))
            block.sync(make("sp"))


class MK:
    def __init__(self, T=SEQ, L=DEPTH, dbg=(), stop_after=None):
        self.T = T
        self.L = L
        self.NTC = T // 512
        self.NTB = T // 128
        self.dbg = set(dbg)
        self.stop_after = stop_after
        nc = bass.Bass("TRN2", target_bir_lowering=False)
        self.nc = nc
        self.S = Sched(nc)
        self.st = contextlib.ExitStack()
        self.dbufs = {}

    def dram_in(self, name, shape):
        return self.nc.dram_tensor(name, list(shape), F32, kind="ExternalInput").ap()

    def scratch(self, name, shape, dtype):
        kind = "ExternalOutput" if name in self.dbg else "Internal"
        return self.nc.dram_tensor(name, list(shape), dtype, kind=kind).ap()

    def sb(self, name, shape, dtype):
        return self.st.enter_context(self.nc.sbuf_tensor(name, list(shape), dtype))

    def db(self, name, key=0):
        k = (name, key)
        if k not in self.dbufs:
            self.dbufs[k] = Buf("%s_%s" % (name, key))
        return self.dbufs[k]

    def vb(self, name=""):
        return Buf(name, big=True)

    def newphase(self):
        S = self.S
        t = self.dummy
        S.add("pool", lambda e: e.memset(t[0:1, 0:1], 0.0), writes=[S.bigtok, self.dummy_b])

    def view(self, off, shape, dtype):
        esz = 2 if dtype == BF16 else 4
        n = 1
        for s in shape[1:]:
            n *= s
        assert off % esz == 0 and off + n * esz <= 131072, (off, shape)
        base = self.big_bf if dtype == BF16 else (self.big_i if dtype == I32 else self.big_f)
        o = off // esz
        ap = base[0:shape[0], o:o + n]
        if len(shape) == 3:
            ap = ap.rearrange("p (a b) -> p a b", a=shape[1])
        elif len(shape) == 4:
            ap = ap.rearrange("p (a b c) -> p a b c", a=shape[1], b=shape[2])
        return ap

    def mm(self, ps, pb, lhsT, rhs, start, stop, reads):
        self.S.add("pe", lambda e: e.matmul(ps, lhsT, rhs, start=start, stop=stop), reads=reads, writes=[pb])

    def tr(self, ps, pb, in_, ident, reads):
        self.S.add("pe", lambda e: e.transpose(ps, in_, ident), reads=reads, writes=[pb])

    def act(self, out, in_, func, reads, writes, bias=None, scale=None):
        kw = {}
        if bias is not None:
            kw["bias"] = bias
        if scale is not None:
            kw["scale"] = scale
        self.S.add("act", lambda e: e.activation(out, in_, func, **kw), reads=reads, writes=writes)

    def tt(self, eng, out, a, b, op, reads, writes):
        self.S.add(eng, lambda e: e.tensor_tensor(out, a, b, op), reads=reads, writes=writes)

    def ts(self, eng, out, a, s1, s2, op0, op1, reads, writes):
        if op1 is None:
            self.S.add(eng, lambda e: e.tensor_scalar(out, a, s1, None, op0), reads=reads, writes=writes)
        else:
            self.S.add(eng, lambda e: e.tensor_scalar(out, a, s1, s2, op0, op1), reads=reads, writes=writes)

    def stt(self, out, a, s, b, op0, op1, reads, writes):
        self.S.add("dve", lambda e: e.scalar_tensor_tensor(out, a, s, b, op0, op1), reads=reads, writes=writes)

    def cp(self, eng, out, in_, reads, writes):
        if eng == "act":
            self.S.add("act", lambda e: e.activation(out, in_, AF.Copy), reads=reads, writes=writes)
        else:
            self.S.add(eng, lambda e: e.tensor_copy(out, in_), reads=reads, writes=writes)

    def memset(self, eng, ap, val, writes, reads=()):
        self.S.add(eng, lambda e: e.memset(ap, val), reads=reads, writes=writes)

    def nps(self):
        i = self.psi % 8
        self.psi += 1
        return self.ps[i], self.pb[i]

    def ntmp(self):
        i = self.tmpi % len(self.tmp)
        self.tmpi += 1
        return self.tmp[i], self.tmpb[i]

    def nstg(self):
        i = self.stgi % 2
        self.stgi += 1
        return self.stg[i], self.stgb[i]

    def nwt(self):
        i = self.wti % 2
        self.wti += 1
        return self.wt[i], self.wtb[i]

    def build(self):
        nc, S, T, L = self.nc, self.S, self.T, self.L
        self.x = self.dram_in("x", [T, D])
        self.norm1_g = self.dram_in("norm1_g", [DEPTH, D])
        self.w_in = self.dram_in("w_in", [DEPTH, D, INW])
        self.mla_q_norm_g = self.dram_in("mla_q_norm_g", [DEPTH, 512])
        self.mla_kv_norm_g = self.dram_in("mla_kv_norm_g", [DEPTH, 256])
        self.mla_w_uq = self.dram_in("mla_w_uq", [DEPTH, 512, 1152])
        self.mla_w_ukv = self.dram_in("mla_w_ukv", [DEPTH, 256, 1536])
        self.fox_b_f = self.dram_in("fox_b_f", [DEPTH, 6])
        self.w_br_fox = self.dram_in("w_br_fox", [DEPTH, 768, D])
        self.w_br_mla = self.dram_in("w_br_mla", [DEPTH, 768, D])
        self.w_br_ret = self.dram_in("w_br_ret", [DEPTH, 1024, D])
        self.w_out = self.dram_in("w_out", [DEPTH, D, D])
        self.norm2_g = self.dram_in("norm2_g", [DEPTH, D])
        self.ffn_w_up = self.dram_in("ffn_w_up", [DEPTH, D, DFF])
        self.ffn_w_gate = self.dram_in("ffn_w_gate", [DEPTH, D, DFF])
        self.ffn_conv_w = self.dram_in("ffn_conv_w", [DEPTH, 3, DFF])
        self.ffn_conv_b = self.dram_in("ffn_conv_b", [DEPTH, DFF])
        self.ffn_w_down = self.dram_in("ffn_w_down", [DEPTH, DFF, D])
        self.final_norm_g = self.dram_in("final_norm_g", [D])
        self.y = nc.dram_tensor("y", [T, D], F32, kind="ExternalOutput").ap()
        sc = self.scratch
        self.xT = sc("xT", [D, T], F32)
        self.QF = sc("QF", [768, T], BF16)
        self.KF = sc("KF", [768, T], BF16)
        self.VF = sc("VF", [T, 768], BF16)
        self.CROW = sc("CROW", [6, T], BF16)
        self.CQ = sc("CQ", [512, T], BF16)
        self.CKV = sc("CKV", [256, T], BF16)
        self.MKR = sc("MKR", [64, T], BF16)
        self.RQ = sc("RQ", [512, T], BF16)
        self.RK = sc("RK", [512, T], BF16)
        self.RV = sc("RV", [T, 1024], BF16)
        self.RG = sc("RG", [1024, T], BF16)
        self.GT = sc("GT", [6144, T], BF16)
        self.MQN = sc("MQN", [768, T], BF16)
        self.MQR = sc("MQR", [384, T], BF16)
        self.MKN = sc("MKN", [768, T], BF16)
        self.MV = sc("MV", [T, 768], BF16)
        self.AO = sc("AO", [768, T], BF16)
        self.BO = sc("BO", [768, T], BF16)
        self.CO = sc("CO", [1024, T], BF16)
        self.MG = sc("MG", [D, T], BF16)
        self.FA = sc("FA", [DFF, T], BF16)
        self.RSTD = sc("RSTD", [T], F32)
        self.rstd_ready = False
        self.R128C = sc("R128C", [128, T], F32)
        self.R128S = sc("R128S", [128, T], F32)
        self.R64C = sc("R64C", [128, T], F32)
        self.R64S = sc("R64S", [128, T], F32)

        with self.st:
            big = self.sb("big", [128, 32768], F32)
            self.big_f = big
            self.big_bf = big.bitcast(BF16)
            self.big_i = big.bitcast(I32)
            self.wt = [self.sb("wt%d" % i, [128, 8192], BF16) for i in range(2)]
            self.wtb = [Buf("wt%d" % i) for i in range(2)]
            self.wt_extra = {id(b): [Buf("wtq%d" % j) for j in range(4)] for b in self.wtb}
            self.stg = [self.sb("stg%d" % i, [128, 4096], BF16) for i in range(2)]
            self.stgb = [Buf("stg%d" % i) for i in range(2)]
            self.tmp = [self.sb("tmp%d" % i, [128, 514], F32) for i in range(8)]
            self.tmpb = [Buf("tmp%d" % i) for i in range(8)]
            self.ps = [self.st.enter_context(nc.psum_tensor("ps%d" % i, [128, 512], F32)) for i in range(8)]
            self.pb = [Buf("ps%d" % i) for i in range(8)]
            self.psi = self.tmpi = self.stgi = self.wti = 0
            self.ident = self.sb("ident", [128, 128], F32)
            self.identb = Buf("ident")
            self.ones_bf = self.sb("ones_bf", [128, 128], BF16)
            self.onesf = self.sb("onesf", [128, 512], F32)
            self.epsc = self.sb("epsc", [128, 1], F32)
            self.constb = Buf("const")
            self.dummy = self.sb("dummyt", [128, 2], F32)
            self.dummy_b = Buf("dummy")
            self.gcol = self.sb("gcol", [128, 16 * 9], F32)
            self.gqc = self.sb("gqc", [128, 4 * 4], F32)
            self.gkvc = self.sb("gkvc", [128, 2 * 4], F32)
            self.cpar = self.sb("cpar", [128, NFC * 16], F32)
            self.fbneg = self.sb("fbneg", [128, 4], F32)
            self.fbias = self.sb("fbias", [128, 32 * 6], F32)
            self.fbiasb = Buf("fbias")
            self.parb = Buf("params")
            self.ccar = self.sb("ccar", [6, 2], F32)
            self.ccar_b = Buf("ccar")

            self.phase_init()
            self.phase_transpose_in()
            for l in range(L):
                self.phase_norm(l, 0)
                if self.stop_after == "norm1":
                    break
                self.phase_inproj(l)
                if self.stop_after == "inproj":
                    break
                self.phase_mla_pre(l)
                if self.stop_after == "mlapre":
                    break
                self.phase_attn(l)
                if self.stop_after == "attn":
                    break
                self.phase_merge(l)
                if self.stop_after == "merge":
                    break
                self.phase_resid_gemm(self.MG, D, self.w_out[l], "wo")
                if self.stop_after == "outproj":
                    break
                self.phase_norm(l, 1)
                self.phase_ffn_up(l)
                if self.stop_after == "ffnup":
                    break
                self.phase_resid_gemm(self.FA, DFF, self.ffn_w_down[l], "wd")
            self.phase_final()
            S.emit()
        return nc

    def phase_init(self):
        S, T = self.S, self.T
        cb = self.constb
        self.memset("pool", self.ident[:], 1.0, [self.identb])
        idt = self.ident
        S.add("pool", lambda e: e.affine_select(idt[:], idt[:], [[-1, 128]], ALU.is_equal, 0.0, base=0,
                                                channel_multiplier=1), reads=[self.identb], writes=[self.identb])
        self.memset("pool", self.ones_bf[:], 1.0, [cb])
        self.memset("pool", self.onesf[:], 1.0, [cb])
        self.memset("pool", self.epsc[:], EPS, [cb])
        self.memset("pool", self.dummy[:], 0.0, [self.dummy_b])
        self.newphase()
        pr = self.view(0, [16, DFF], F32)
        prb = self.vb("pr")
        pr2 = self.view(DFF * 4, [16, 512], F32)
        pr2b = self.vb("pr2")
        self.memset("pool", pr[:, :], 0.0, [prb])
        self.memset("pool", pr2[:, :], 0.0, [pr2b])
        S.dma("sp", pr[0:4, 0:D], self.norm1_g, writes=[prb])
        S.dma("sp", pr[4:8, 0:D], self.norm2_g, writes=[prb])
        S.dma("sp", pr[8:9, 0:D], self.final_norm_g.rearrange("(o d) -> o d", o=1), writes=[prb])
        ps, pb = self.nps()
        for kc in range(16):
            self.tr(ps[:, kc * 9:(kc + 1) * 9], pb, pr[0:9, kc * 128:(kc + 1) * 128], self.ident[0:9, 0:9],
                    [prb, self.identb])
        self.cp("dve", self.gcol[:, :], ps[:, 0:144], [pb], [self.parb])
        S.dma("sp", pr2[0:4, 0:512], self.mla_q_norm_g, writes=[pr2b])
        ps, pb = self.nps()
        for kc in range(4):
            self.tr(ps[:, kc * 4:(kc + 1) * 4], pb, pr2[0:4, kc * 128:(kc + 1) * 128], self.ident[0:4, 0:4],
                    [pr2b, self.identb])
        self.cp("dve", self.gqc[:, :], ps[:, 0:16], [pb], [self.parb])
        pr3 = self.view(DFF * 4 + 2048, [16, 256], F32)
        pr3b = self.vb("pr3")
        S.dma("sp", pr3[0:4, 0:256], self.mla_kv_norm_g, writes=[pr3b])
        ps, pb = self.nps()
        for kc in range(2):
            self.tr(ps[:, kc * 4:(kc + 1) * 4], pb, pr3[0:4, kc * 128:(kc + 1) * 128], self.ident[0:4, 0:4],
                    [pr3b, self.identb])
        self.cp("dve", self.gkvc[:, :], ps[:, 0:8], [pb], [self.parb])
        pr4 = self.view(DFF * 4 + 4096, [16, 8], F32)
        pr4b = self.vb("pr4")
        S.dma("sp", pr4[0:4, 0:6], self.fox_b_f, writes=[pr4b])
        ps, pb = self.nps()
        self.tr(ps[0:6, 0:4], pb, pr4[0:4, 0:6], self.ident[0:4, 0:4], [pr4b, self.identb])
        self.ts("dve", self.fbneg[0:6, 0:4], ps[0:6, 0:4], -1.0, None, ALU.mult, None, [pb], [self.parb])
        self.newphase()
        prc = self.view(0, [16, DFF], F32)
        prcb = self.vb("prc")
        S.dma("sp", prc[0:12, :], self.ffn_conv_w.rearrange("l i f -> (l i) f"), writes=[prcb])
        S.dma("sp", prc[12:16, :], self.ffn_conv_b, writes=[prcb])
        psA, pbA = self.nps()
        psB, pbB = self.nps()
        for fc in range(NFC):
            if fc < 32:
                self.tr(psA[:, fc * 16:(fc + 1) * 16], pbA, prc[0:16, fc * 128:(fc + 1) * 128],
                        self.ident[0:16, 0:16], [prcb, self.identb])
            else:
                self.tr(psB[:, (fc - 32) * 16:(fc - 31) * 16], pbB, prc[0:16, fc * 128:(fc + 1) * 128],
                        self.ident[0:16, 0:16], [prcb, self.identb])
        self.cp("dve", self.cpar[:, 0:512], psA[:, :], [pbA], [self.parb])
        self.cp("dve", self.cpar[:, 512:704], psB[:, 0:192], [pbB], [self.parb])
        self.newphase()
        TW = T
        pos = self.view(0, [128, TW], F32)
        yv = self.view(TW * 4, [128, TW], F32)
        yi = self.view(TW * 8, [128, TW], I32)
        yf = self.view(TW * 12, [128, TW], F32)
        ff = self.view(TW * 16, [128, TW], F32)
        mm_ = self.view(TW * 20, [128, TW], F32)
        tab = self.view(TW * 24, [128, TW], F32)
        posb, yb, yib, yfb, fb, mb, tabb = [self.vb(n) for n in ("pos", "y", "yi", "yf", "f", "m", "tab")]
        S.add("pool", lambda e: e.iota(pos, [[1, TW]], base=0, channel_multiplier=0,
                                       allow_small_or_imprecise_dtypes=True), writes=[posb])
        fidx = self.sb("fidx", [128, 1], F32)
        invf = self.sb("invf", [128, 1], F32)
        fidxb = Buf("fidx")
        invfb = Buf("invf")
        for (dd, nf, Cd, Sd) in ((128, 64, self.R128C, self.R128S), (64, 32, self.R64C, self.R64S)):
            for r in range(128 // nf):
                S.add("pool", lambda e, r=r, nf=nf: e.iota(fidx[r * nf:(r + 1) * nf, :], [[0, 1]], base=0,
                                                           channel_multiplier=1,
                                                           allow_small_or_imprecise_dtypes=True),
                      writes=[fidxb])
            self.act(invf[:, :], fidx[:, :], AF.Exp, [fidxb], [invfb], scale=-math.log(10000.0) * 2.0 / dd)
            for (shift, dst, nm) in ((0.0, Sd, "s"), (0.25, Cd, "c")):
                self.ts("dve", yv, pos, invf[:, 0:1], 1.0 / (2.0 * math.pi), ALU.mult, ALU.mult, [posb, invfb], [yb])
                if shift:
                    self.ts("dve", yv, yv, shift, None, ALU.add, None, [yb], [yb])
                self.cp("dve", yi, yv, [yb], [yib])
                self.cp("dve", yf, yi, [yib], [yfb])
                self.tt("dve", ff, yv, yf, ALU.subtract, [yb, yfb], [fb])
                self.ts("dve", mm_, ff, 0.5, None, ALU.is_gt, None, [fb], [mb])
                self.tt("dve", ff, ff, mm_, ALU.subtract, [fb, mb], [fb])
                self.ts("dve", mm_, ff, -0.5, None, ALU.is_lt, None, [fb], [mb])
                self.tt("dve", ff, ff, mm_, ALU.add, [fb, mb], [fb])
                self.act(tab, ff, AF.Sin, [fb], [tabb], scale=2.0 * math.pi)
                S.dma("sp", dst, tab, reads=[tabb], writes=[self.db("rope", nm + str(dd))])

    def phase_transpose_in(self):
        S, T = self.S, self.T
        self.newphase()
        xin = [self.view(i * 32768, [128, 4, D], F32) for i in range(2)]
        xinb = [self.vb("xin%d" % i) for i in range(2)]
        xo = [self.view(65536 + i * 32768, [128, KC, 512], F32) for i in range(2)]
        xob = [self.vb("xo%d" % i) for i in range(2)]
        for tc in range(self.NTC):
            xi, xib = xin[tc % 2], xinb[tc % 2]
            xo_, xob_ = xo[tc % 2], xob[tc % 2]
            S.dma("sp", xi, self.x[tc * 512:(tc + 1) * 512, :].rearrange("(tb p) d -> p tb d", p=128), writes=[xib])
            for kc in range(KC):
                ps, pb = self.nps()
                for tb in range(4):
                    self.tr(ps[:, tb * 128:(tb + 1) * 128], pb, xi[:, tb, kc * 128:(kc + 1) * 128], self.ident[:, :],
                            [xib, self.identb])
                self.cp("act" if kc % 2 else "dve", xo_[:, kc, :], ps[:, :], [pb], [xob_])
            S.dma("sp", self.xT[:, tc * 512:(tc + 1) * 512].rearrange("(kc p) t -> p kc t", p=128), xo_,
                  reads=[xob_], writes=[self.db("xT", (kc, tc)) for kc in range(KC)])

    def rstd_from_ps(self, ps, pb, n, dst, dstb, width=512):
        self.act(dst, ps, AF.Ln, [pb, self.constb], [dstb], bias=self.epsc[:, 0:1], scale=1.0 / n)
        self.act(dst, dst, AF.Exp, [dstb], [dstb], scale=-0.5)

    def phase_norm(self, l, which):
        S, T = self.S, self.T
        self.newphase()
        hT = self.view(0, [128, KC, T], BF16)
        self.hT = hT
        self.hTb = [self.vb("hT%d" % tc) for tc in range(self.NTC)]
        gidx = l if which == 0 else 4 + l
        for hc in range(T // 256):
            tc = hc // 2
            wt, wtb = self.nwt()
            qbs = self.wt_extra[id(wtb)]
            xs = wt.bitcast(F32)[:, 0:4096].rearrange("p (k t) -> p k t", k=KC)
            for q in range(4):
                S.dma("sp", xs[:, q * 4:(q + 1) * 4, :],
                      self.xT[q * 512:(q + 1) * 512, hc * 256:(hc + 1) * 256].rearrange("(k p) t -> p k t", p=128),
                      reads=[self.db("xT", (q * 4 + a, tc)) for a in range(4)], writes=[wtb, qbs[q]])
            rs, rsb = self.ntmp()
            if self.rstd_ready:
                S.dma("sp", rs[:, 0:256], self.RSTD[hc * 256:(hc + 1) * 256].partition_broadcast(128),
                      reads=[self.db("RSTD", tc)], writes=[rsb])
            else:
                ps, pb = self.nps()
                for q in range(4):
                    tmp, tmpb = self.ntmp()
                    sq4 = tmp.bitcast(BF16)[:, 0:1024].rearrange("p (a t) -> p a t", a=4)
                    self.act(sq4, xs[:, q * 4:(q + 1) * 4, :], AF.Square, [qbs[q]], [tmpb])
                    for a in range(4):
                        kc = q * 4 + a
                        self.mm(ps[:, 0:256], pb, self.ones_bf[:, :], sq4[:, a, :], kc == 0, kc == KC - 1,
                                [tmpb, self.constb])
                self.rstd_from_ps(ps[:, 0:256], pb, float(D), rs[:, 0:256], rsb)
            for kc in range(KC):
                self.stt(hT[:, kc, hc * 256:(hc + 1) * 256], xs[:, kc, :],
                         self.gcol[:, kc * 9 + gidx:kc * 9 + gidx + 1], rs[:, 0:256], ALU.mult, ALU.mult,
                         [qbs[kc // 4], rsb, self.parb], [self.hTb[tc]])

    def load_w(self, view_fn, dmas):
        wt, wtb = self.nwt()
        for (o, i) in dmas:
            self.S.dma("pool", o(wt), i, writes=[wtb] + self.wt_extra[id(wtb)])
        return wt, wtb

    def ws_group(self, ps, pb, wv, wtb, c0, M, at, atb_fn, kcn, tc):
        for kc in range(kcn):
            self.mm(ps[0:M, :], pb, wv[:, kc, c0:c0 + M], at[:, kc, tc * 512:(tc + 1) * 512], kc == 0, kc == kcn - 1,
                    [wtb, atb_fn(tc)])

    def as_tile(self, wv, wtb, c0, ncols, at, atb_fn, kcn, dst, dstkey, dcol0):
        S = self.S
        stg = stgb = None
        for tb in range(self.NTB):
            if tb % 8 == 0:
                stg, stgb = self.nstg()
            sv = stg[:, :].rearrange("p (a b) -> p a b", a=8)
            ps, pb = self.nps()
            for kc in range(kcn):
                self.mm(ps[:, 0:ncols], pb, at[:, kc, tb * 128:(tb + 1) * 128], wv[:, kc, c0:c0 + ncols], kc == 0,
                        kc == kcn - 1, [wtb, atb_fn(tb // 4)])
            self.cp("act" if tb % 2 else "dve", sv[:, tb % 8, 0:ncols], ps[:, 0:ncols], [pb], [stgb])
            if tb % 8 == 7:
                t0 = (tb - 7) * 128
                S.dma("sp", dst[t0:t0 + 1024, dcol0:dcol0 + ncols].rearrange("(a p) n -> p a n", p=128),
                      sv[:, :, 0:ncols], reads=[stgb], writes=[self.db(dstkey, (dcol0, tb // 8))])

    def rope_pair(self, psA, pbA, psB, pbB, M, Ct, Ctb, St, Stb, o1, o1b, o2, o2b, tc):
        t1, t1b = self.ntmp()
        t2, t2b = self.ntmp()
        t3, t3b = self.ntmp()
        t4, t4b = self.ntmp()
        sl = slice(tc * 512, (tc + 1) * 512)
        self.tt("dve", t1[0:M, 0:512], psA[0:M, :], Ct[0:M, 0:512], ALU.mult, [pbA, Ctb], [t1b])
        self.tt("dve", t2[0:M, 0:512], psB[0:M, :], St[0:M, 0:512], ALU.mult, [pbB, Stb], [t2b])
        self.tt("dve", t3[0:M, 0:512], psB[0:M, :], Ct[0:M, 0:512], ALU.mult, [pbB, Ctb], [t3b])
        self.tt("dve", t4[0:M, 0:512], psA[0:M, :], St[0:M, 0:512], ALU.mult, [pbA, Stb], [t4b])
        self.tt("dve", o1[0:M, sl], t1[0:M, 0:512], t2[0:M, 0:512], ALU.subtract, [t1b, t2b], [o1b])
        self.tt("dve", o2[0:M, sl], t3[0:M, 0:512], t4[0:M, 0:512], ALU.add, [t3b, t4b], [o2b])

    def load_rope_tabs(self, dd, tc):
        Cd, Sd = (self.R128C, self.R128S) if dd == 128 else (self.R64C, self.R64S)
        Ct, Ctb = self.ntmp()
        St, Stb = self.ntmp()
        self.S.dma("sp", Ct[:, 0:512], Cd[:, tc * 512:(tc + 1) * 512], reads=[self.db("rope", "c" + str(dd))],
                   writes=[Ctb])
        self.S.dma("sp", St[:, 0:512], Sd[:, tc * 512:(tc + 1) * 512], reads=[self.db("rope", "s" + str(dd))],
                   writes=[Stb])
        return Ct, Ctb, St, Stb

    def plain_job(self, wv, wtb, c0, M, at, atb_fn, kcn, dst, dstkey, row0, func, evi=0):
        stg, stgb = self.nstg()
        for tc in range(self.NTC):
            ps, pb = self.nps()
            self.ws_group(ps, pb, wv, wtb, c0, M, at, atb_fn, kcn, tc)
            o = stg[0:M, tc * 512:(tc + 1) * 512]
            if func is None:
                self.cp("act" if (tc + evi) % 2 else "dve", o, ps[0:M, :], [pb], [stgb])
            else:
                self.act(o, ps[0:M, :], func, [pb], [stgb])
        self.S.dma("sp", dst[row0:row0 + M, :], stg[0:M, 0:self.T], reads=[stgb], writes=[self.db(dstkey, row0)])

    def phase_inproj(self, l):
        S, T = self.S, self.T
        W = self.w_in[l]
        hT = self.hT
        hb = lambda tc: self.hTb[tc]

        def wtile(c0, nc_):
            return self.load_w(None, [(lambda wt: wt[:, 0:KC * nc_].rearrange("p (k n) -> p k n", k=KC),
                                       W[:, c0:c0 + nc_].rearrange("(k p) n -> p k n", p=128))])

        def wview(wt, nc_):
            return wt[:, 0:KC * nc_].rearrange("p (k n) -> p k n", k=KC)

        def seg(c0, n, dst, key, func):
            o = 0
            while o < n:
                nc_ = min(512, n - o)
                wt, wtb = wtile(c0 + o, nc_)
                wv = wview(wt, nc_)
                for j in range(0, nc_, 128):
                    M = min(128, nc_ - j)
                    self.plain_job(wv, wtb, j, M, hT, hb, KC, dst, key, o + j, func, evi=j // 128)
                o += nc_

        def seg_as(c0, n, dst, key):
            o = 0
            while o < n:
                nc_ = min(512, n - o)
                wt, wtb = wtile(c0 + o, nc_)
                wv = wview(wt, nc_)
                self.as_tile(wv, wtb, 0, nc_, hT, hb, KC, dst, key, o)
                o += nc_

        seg(O_FQ, 768, self.QF, "QF", None)
        seg(O_FK, 768, self.KF, "KF", None)
        seg_as(O_FV, 768, self.VF, "VF")
        wt, wtb = wtile(O_FF, 6)
        wv = wview(wt, 6)
        self.memset("dve", self.ccar[0:6, :], 0.0, [self.ccar_b])
        crow_st, crow_b = self.nstg()
        for tc in range(self.NTC):
            ps, pb = self.nps()
            self.ws_group(ps, pb, wv, wtb, 0, 6, hT, hb, KC, tc)
            e1, e1b = self.ntmp()
            self.act(e1[0:6, 0:512], ps[0:6, :], AF.Exp, [pb, self.parb], [e1b], bias=self.fbneg[0:6, l:l + 1],
                     scale=-1.0)
            self.act(e1[0:6, 0:512], e1[0:6, 0:512], AF.Ln, [e1b, self.constb], [e1b], bias=self.onesf[0:6, 0:1])
            c1, c1b = self.ntmp()
            car = self.ccar
            S.add("dve", lambda e, c1=c1, e1=e1, p=tc % 2: e.tensor_tensor_scan(
                c1[0:6, 0:512], self.onesf[0:6, 0:512], e1[0:6, 0:512], car[0:6, p:p + 1], ALU.mult, ALU.add),
                reads=[e1b, self.constb, self.ccar_b], writes=[c1b])
            self.cp("dve", car[0:6, (tc + 1) % 2:(tc + 1) % 2 + 1], c1[0:6, 511:512], [c1b], [self.ccar_b])
            self.act(crow_st[0:6, tc * 512:(tc + 1) * 512], c1[0:6, 0:512], AF.Copy, [c1b], [crow_b],
                     scale=-math.sqrt(128.0))
            ps2, pb2 = self.nps()
            for j in range(4):
                self.tr(ps2[:, j * 6:(j + 1) * 6], pb2, c1[0:6, j * 128:(j + 1) * 128], self.ident[0:6, 0:6],
                        [c1b, self.identb])
            self.cp("dve", self.fbias[:, tc * 24:(tc + 1) * 24], ps2[:, 0:24], [pb2], [self.fbiasb])
        S.dma("sp", self.CROW[:, :], crow_st[0:6, 0:T], reads=[crow_b], writes=[self.db("CROW")])
        seg(O_MQ, 512, self.CQ, "CQ", None)
        seg(O_MKV, 256, self.CKV, "CKV", None)
        wt, wtb = wtile(O_MKR, 64)
        wv = wview(wt, 64)
        o1, o1b = self.nstg()
        o2, o2b = self.nstg()
        for tc in range(self.NTC):
            Ct, Ctb, St, Stb = self.load_rope_tabs(64, tc)
            psA, pbA = self.nps()
            self.ws_group(psA, pbA, wv, wtb, 0, 32, hT, hb, KC, tc)
            psB, pbB = self.nps()
            self.ws_group(psB, pbB, wv, wtb, 32, 32, hT, hb, KC, tc)
            self.rope_pair(psA, pbA, psB, pbB, 32, Ct, Ctb, St, Stb, o1, o1b, o2, o2b, tc)
        S.dma("sp", self.MKR[0:32, :], o1[0:32, 0:T], reads=[o1b], writes=[self.db("MKR", 0)])
        S.dma("sp", self.MKR[32:64, :], o2[0:32, 0:T], reads=[o2b], writes=[self.db("MKR", 1)])
        for (c0, dst, key) in ((O_RQ, self.RQ, "RQ"), (O_RK, self.RK, "RK")):
            dm = []
            for pr_ in range(2):
                for two in range(2):
                    for hp in range(2):
                        cs = c0 + pr_ * 256 + hp * 128 + two * 64
                        src = W[:, cs:cs + 64].rearrange("(k p) j -> p k j", p=128)
                        off = (pr_ * 2 + two) * 128 + hp * 64
                        dm.append((lambda wt, off=off: wt[:, 0:KC * 512].rearrange("p (k n) -> p k n", k=KC)[
                            :, :, off:off + 64], src))
            wt, wtb = self.load_w(None, dm)
            wv = wview(wt, 512)
            for pr_ in range(2):
                o1, o1b = self.nstg()
                o2, o2b = self.nstg()
                for tc in range(self.NTC):
                    Ct, Ctb, St, Stb = self.load_rope_tabs(128, tc)
                    psA, pbA = self.nps()
                    self.ws_group(psA, pbA, wv, wtb, pr_ * 256, 128, hT, hb, KC, tc)
                    psB, pbB = self.nps()
                    self.ws_group(psB, pbB, wv, wtb, pr_ * 256 + 128, 128, hT, hb, KC, tc)
                    self.rope_pair(psA, pbA, psB, pbB, 128, Ct, Ctb, St, Stb, o1, o1b, o2, o2b, tc)
                dv = dst.rearrange("(h two j) t -> two h j t", two=2, j=64)
                for hp in range(2):
                    h = pr_ * 2 + hp
                    S.dma("sp", dv[0, h], o1[hp * 64:(hp + 1) * 64, 0:T], reads=[o1b], writes=[self.db(key, (h, 0))])
                    S.dma("sp", dv[1, h], o2[hp * 64:(hp + 1) * 64, 0:T], reads=[o2b], writes=[self.db(key, (h, 1))])
        seg_as(O_RV, 1024, self.RV, "RV")
        seg(O_RG, 1024, self.RG, "RG", AF.Silu)
        seg(O_GT, 6144, self.GT, "GT", AF.Sigmoid)

    def phase_mla_pre(self, l):
        S, T = self.S, self.T
        self.newphase()
        NTC = self.NTC
        cq = self.view(0, [128, 4, T], BF16)
        cqn = self.view(8 * T, [128, 4, T], BF16)
        ckv = self.view(16 * T, [128, 2, T], BF16)
        ckvn = self.view(20 * T, [128, 2, T], BF16)
        cqb, cqnb, ckvb, ckvnb = self.vb("cq"), [self.vb("cqn%d" % i) for i in range(NTC)], self.vb("ckv"), \
            [self.vb("ckvn%d" % i) for i in range(NTC)]
        S.dma("sp", cq, self.CQ.rearrange("(k p) t -> p k t", p=128), reads=[self.db("CQ", r) for r in range(0, 512, 128)],
              writes=[cqb])
        S.dma("sp", ckv, self.CKV.rearrange("(k p) t -> p k t", p=128),
              reads=[self.db("CKV", r) for r in range(0, 256, 128)], writes=[ckvb])
        Wq = self.mla_w_uq[l].rearrange("(k p) (h c) -> p k h c", p=128, c=192)
        dm = []
        for k in range(4):
            for (o0, c0_, c1_) in ((0, 0, 128), (768, 128, 160), (960, 160, 192)):
                w_ = c1_ - c0_
                dm.append((lambda wt, k=k, o0=o0, w_=w_: wt[:, 0:4 * 1152].rearrange("p (k n) -> p k n", k=4)[
                    :, k, o0:o0 + 6 * w_].rearrange("p (h c) -> p h c", h=6), Wq[:, k, :, c0_:c1_]))
        wq, wqb = self.load_w(None, dm)
        wqv = wq[:, 0:4 * 1152].rearrange("p (k n) -> p k n", k=4)
        Wkv = self.mla_w_ukv[l].rearrange("(k p) (h c) -> p k h c", p=128, c=256)
        dm = []
        for k in range(2):
            for (o0, c0_) in ((0, 0), (768, 128)):
                dm.append((lambda wt, k=k, o0=o0: wt[:, 0:2 * 1536].rearrange("p (k n) -> p k n", k=2)[
                    :, k, o0:o0 + 768].rearrange("p (h c) -> p h c", h=6), Wkv[:, k, :, c0_:c0_ + 128]))
        wk, wkb = self.load_w(None, dm)
        wkv = wk[:, 0:2 * 1536].rearrange("p (k n) -> p k n", k=2)
        for (src, srcb, dstv, dstb, nk, gc, n) in ((cq, cqb, cqn, cqnb, 4, self.gqc, 512.0),
                                                    (ckv, ckvb, ckvn, ckvnb, 2, self.gkvc, 256.0)):
            for tc in range(NTC):
                ps, pb = self.nps()
                for kc in range(nk):
                    tmp, tmpb = self.ntmp()
                    sq = tmp.bitcast(BF16)[:, 0:512]
                    self.act(sq, src[:, kc, tc * 512:(tc + 1) * 512], AF.Square, [srcb], [tmpb])
                    self.mm(ps[:, :], pb, self.ones_bf[:, :], sq, kc == 0, kc == nk - 1, [tmpb, self.constb])
                rs, rsb = self.ntmp()
                self.rstd_from_ps(ps[:, :], pb, n, rs[:, 0:512], rsb)
                for kc in range(nk):
                    self.stt(dstv[:, kc, tc * 512:(tc + 1) * 512], src[:, kc, tc * 512:(tc + 1) * 512],
                             gc[:, kc * 4 + l:kc * 4 + l + 1], rs[:, 0:512], ALU.mult, ALU.mult,
                             [srcb, rsb, self.parb], [dstb[tc]])
        qb = lambda tc: cqnb[tc]
        kb_ = lambda tc: ckvnb[tc]
        for h in range(6):
            self.plain_job(wqv, wqb, h * 128, 128, cqn, qb, 4, self.MQN, "MQN", h * 128, None, evi=h)
        mqr = self.MQR.rearrange("(h two j) t -> two h j t", two=2, j=32)
        for (a0, b0, M, h0) in ((768, 960, 128, 0), (896, 1088, 64, 4)):
            o1, o1b = self.nstg()
            o2, o2b = self.nstg()
            for tc in range(NTC):
                Ct, Ctb, St, Stb = self.load_rope_tabs(64, tc)
                psA, pbA = self.nps()
                self.ws_group(psA, pbA, wqv, wqb, a0, M, cqn, qb, 4, tc)
                psB, pbB = self.nps()
                self.ws_group(psB, pbB, wqv, wqb, b0, M, cqn, qb, 4, tc)
                self.rope_pair(psA, pbA, psB, pbB, M, Ct, Ctb, St, Stb, o1, o1b, o2, o2b, tc)
            nh = M // 32
            for hh in range(nh):
                S.dma("sp", mqr[0, h0 + hh], o1[hh * 32:(hh + 1) * 32, 0:T], reads=[o1b],
                      writes=[self.db("MQR", (h0 + hh, 0))])
                S.dma("sp", mqr[1, h0 + hh], o2[hh * 32:(hh + 1) * 32, 0:T], reads=[o2b],
                      writes=[self.db("MQR", (h0 + hh, 1))])
        for h in range(6):
            self.plain_job(wkv, wkb, h * 128, 128, ckvn, kb_, 2, self.MKN, "MKN", h * 128, None, evi=h)
        self.as_tile(wkv, wkb, 768, 512, ckvn, kb_, 2, self.MV, "MV", 0)
        self.as_tile(wkv, wkb, 1280, 256, ckvn, kb_, 2, self.MV, "MV", 512)

    def attn_head(self, kind, hidx, slot, kparts, qparts, vsrc, vkeys, dv, dst, dstkey, row0, extra):
        S, T, NTC = self.S, self.T, self.NTC
        nvc = dv // 128
        base = slot * 40960
        off = base
        ktiles, qtiles = [], []
        hb = self.slotb[slot]
        for (ap, kd, deps) in kparts:
            kt = self.view(off, [128, T], BF16)
            off += 2 * T
            S.dma("sp", kt[0:kd, :], ap, reads=deps, writes=[hb])
            ktiles.append((kt, kd))
        for (ap, kd, deps) in qparts:
            qt = self.view(off, [128, T], BF16)
            off += 2 * T
            S.dma("sp", qt[0:kd, :], ap, reads=deps, writes=[hb])
            qtiles.append((qt, kd))
        vt = self.view(off, [128, self.NTB, dv], BF16)
        off += 2 * self.NTB * dv
        S.dma("sp", vt, vsrc.rearrange("(kb p) d -> p kb d", p=128), reads=vkeys, writes=[hb])
        crow = None
        if kind == "fox":
            crow = self.view(off, [1, T], BF16)
            off += 2 * T
            S.dma("sp", crow, self.CROW[hidx:hidx + 1, :], reads=[self.db("CROW")], writes=[hb])
        assert off - base <= 40960
        ostg = [self.nstg() for _ in range(nvc)]
        tiles = [(qc, kb) for qc in range(NTC) for kb in range(4 * qc + 4)]
        pts = self.pts
        ptb = self.ptb

        if kind == "ret":
            sbanks, obank, nbank = [0, 1, 6], (lambda par, c: 2 + 2 * par + c), (lambda par: 7)
        else:
            sbanks, obank, nbank = [0, 1, 3, 5], (lambda par, c: 2 + 2 * par), (lambda par: 6 + par)
        nsb = len(sbanks)

        def s_mm(i):
            qc, kb = tiles[i]
            ps, pb = self.ps[sbanks[i % nsb]], self.pb[sbanks[i % nsb]]
            c0 = max(kb - 4 * qc, 0) * 128
            n = len(ktiles) + (1 if crow is not None else 0)
            j = 0
            for (kt, kd), (qt, _) in zip(ktiles, qtiles):
                self.mm(ps[:, c0:512], pb, kt[0:kd, kb * 128:(kb + 1) * 128],
                        qt[0:kd, qc * 512 + c0:(qc + 1) * 512], j == 0, j == n - 1, [hb])
                j += 1
            if crow is not None:
                self.mm(ps[:, c0:512], pb, self.ones_bf[0:1, 0:128], crow[0:1, qc * 512 + c0:(qc + 1) * 512], False,
                        True, [hb, self.constb])

        def transform(i):
            qc, kb = tiles[i]
            ps, pb = self.ps[sbanks[i % nsb]], self.pb[sbanks[i % nsb]]
            pt, ptb_ = pts[i % 4], ptb[i % 4]
            jd = kb - 4 * qc
            c0 = max(jd, 0) * 128
            if kind == "fox":
                self.act(pt[:, c0:512], ps[:, c0:512], AF.Exp, [pb, self.fbiasb], [ptb_],
                         bias=self.fbias[:, kb * 6 + hidx:kb * 6 + hidx + 1], scale=128.0 ** -0.5)
            elif kind == "mla":
                self.act(pt[:, c0:512], ps[:, c0:512], AF.Exp, [pb], [ptb_], scale=192.0 ** -0.5)
            else:
                tb_, tbb, dd_, ddb = extra
                u0 = 384 + qc * 512 - kb * 128
                if jd < 0:
                    self.tt("dve", pt[:, :], ps[:, :], tb_[:, u0:u0 + 512], ALU.mult, [pb, tbb], [ptb_])
                else:
                    c1 = (jd + 1) * 128
                    self.tt("dve", pt[:, jd * 128:c1], ps[:, jd * 128:c1], dd_[:, :], ALU.mult, [pb, ddb], [ptb_])
                    if c1 < 512:
                        self.tt("dve", pt[:, c1:512], ps[:, c1:512], tb_[:, u0 + c1:u0 + 512], ALU.mult, [pb, tbb],
                                [ptb_])
            if jd >= 0:
                dsl = pt[:, jd * 128:(jd + 1) * 128]
                if kind == "fox":
                    S.add("pool", lambda e, dsl=dsl: e.affine_select(dsl, dsl, [[1, 128]], ALU.is_ge, 0.0, base=0,
                                                                     channel_multiplier=-1),
                          reads=[ptb_], writes=[ptb_])
                elif kind == "mla":
                    self.memset("pool", pt[64:128, jd * 128:jd * 128 + 64], 0.0, [ptb_])

        def pv_mm(i):
            qc, kb = tiles[i]
            pt, ptb_ = pts[i % 4], ptb[i % 4]
            first, last = kb == 0, kb == 4 * qc + 3
            par = qc % 2
            c0 = max(kb - 4 * qc, 0) * 128
            for c in range(nvc):
                po, pob = self.ps[obank(par, c)], self.pb[obank(par, c)]
                self.mm(po[:, c0:512], pob, vt[:, kb, c * 128:(c + 1) * 128], pt[:, c0:512], first, last, [hb, ptb_])
            if kind != "ret":
                pn, pnb = self.ps[nbank(par)], self.pb[nbank(par)]
                self.mm(pn[:, c0:512], pnb, self.ones_bf[:, :], pt[:, c0:512], first, last, [ptb_, self.constb])
            if last:
                epilogue(qc)

        def epilogue(qc):
            par = qc % 2
            sl = slice(qc * 512, (qc + 1) * 512)
            if kind != "ret":
                pn, pnb = self.ps[nbank(par)], self.pb[nbank(par)]
                po, pob = self.ps[obank(par, 0)], self.pb[obank(par, 0)]
                rc, rcb = self.ntmp()
                S.add("dve", lambda e: e.reciprocal(rc[:, 0:512], pn[:, :]), reads=[pnb], writes=[rcb])
                self.tt("dve", ostg[0][0][:, sl], po[:, :], rc[:, 0:512], ALU.mult, [pob, rcb], [ostg[0][1]])
            else:
                pn, pnb = self.ps[nbank(par)], self.pb[nbank(par)]
                for c in range(2):
                    po, pob = self.ps[obank(par, c)], self.pb[obank(par, c)]
                    tmp, tmpb = self.ntmp()
                    sq = tmp.bitcast(BF16)[:, 0:512]
                    self.act(sq, po[:, :], AF.Square, [pob], [tmpb])
                    self.mm(pn[:, :], pnb, self.ones_bf[:, :], sq, c == 0, c == 1, [tmpb, self.constb])
                rs, rsb = self.ntmp()
                self.rstd_from_ps(pn[:, :], pnb, 256.0, rs[:, 0:512], rsb)
                for c in range(2):
                    po, pob = self.ps[obank(par, c)], self.pb[obank(par, c)]
                    g, gb = self.ntmp()
                    gv = g.bitcast(BF16)[:, 0:512]
                    r0 = hidx * 256 + c * 128
                    S.dma("sp", gv, self.RG[r0:r0 + 128, sl], reads=[self.db("RG", r0)], writes=[gb])
                    t, tb2 = self.ntmp()
                    self.tt("dve", t[:, 0:512], po[:, :], rs[:, 0:512], ALU.mult, [pob, rsb], [tb2])
                    self.tt("pool", ostg[c][0][:, sl], t[:, 0:512], gv, ALU.mult, [tb2, gb], [ostg[c][1]])

        n = len(tiles)
        s_mm(0)
        if n > 1:
            s_mm(1)
        for i in range(n):
            if i + 2 < n:
                s_mm(i + 2)
            transform(i)
            pv_mm(i)
        for c in range(nvc):
            S.dma("sp", dst[row0 + c * 128:row0 + (c + 1) * 128, :], ostg[c][0][:, 0:T], reads=[ostg[c][1]],
                  writes=[self.db(dstkey, row0 + c * 128)])

    def phase_attn(self, l):
        S, T = self.S, self.T
        self.newphase()
        pbase = 81920
        self.pts = [self.view(pbase + i * 1024, [128, 512], BF16) for i in range(4)]
        self.ptb = [self.vb("pt%d" % i) for i in range(4)]
        self.slotb = [self.vb("slot%d" % i) for i in range(2)]
        slot = 0
        for h in range(6):
            self.attn_head("fox", h, slot % 2,
                           [(self.KF[h * 128:(h + 1) * 128, :], 128, [self.db("KF", h * 128)])],
                           [(self.QF[h * 128:(h + 1) * 128, :], 128, [self.db("QF", h * 128)])],
                           self.VF[:, h * 128:(h + 1) * 128],
                           [self.db("VF", (c, t)) for c in (0, 512) for t in range(self.NTB // 8)],
                           128, self.AO, "AO", h * 128, None)
            slot += 1
        for h in range(6):
            self.attn_head("mla", h, slot % 2,
                           [(self.MKN[h * 128:(h + 1) * 128, :], 128, [self.db("MKN", h * 128)]),
                            (self.MKR[:, :], 64, [self.db("MKR", 0), self.db("MKR", 1)])],
                           [(self.MQN[h * 128:(h + 1) * 128, :], 128, [self.db("MQN", h * 128)]),
                            (self.MQR[h * 64:(h + 1) * 64, :], 64,
                             [self.db("MQR", (h, 0)), self.db("MQR", (h, 1))])],
                           self.MV[:, h * 128:(h + 1) * 128],
                           [self.db("MV", (c, t)) for c in (0, 512) for t in range(self.NTB // 8)],
                           128, self.BO, "BO", h * 128, None)
            slot += 1
        TWD = 384 + T
        ebase = pbase + 4096
        E = self.view(ebase, [128, TWD], F32)
        Eb = self.vb("E")
        S.add("pool", lambda e: e.iota(E, [[1, TWD]], base=-384, channel_multiplier=-1,
                                       allow_small_or_imprecise_dtypes=True), writes=[Eb])
        Ed = self.sb("Ed_%d" % l, [128, 128], F32) if l == 0 else self.Ed
        self.Ed = Ed
        Edb = Buf("Ed")
        if l == 0:
            self.Edb = Edb
            S.add("pool", lambda e: e.iota(Ed[:, :], [[1, 128]], base=0, channel_multiplier=-1,
                                           allow_small_or_imprecise_dtypes=True), writes=[Edb])
            self.act(Ed[:, :], Ed[:, :], AF.Abs, [Edb], [Edb])
            self.Dd = self.sb("Dd", [128, 128], F32)
            self.lgb = self.sb("lgb", [128, 1], F32)
            self.memset("pool", self.lgb[:, :], math.log(128.0 ** -0.5), [Edb])
        Edb = self.Edb
        Tb = self.view(ebase + TWD * 4, [128, TWD], F32)
        Tbb = self.vb("Tb")
        Ddb = self.vb("Dd")
        for h in range(4):
            lg = math.log(1.0 - 2.0 ** (-5.0 - h))
            self.act(Tb, E, AF.Exp, [Eb, Edb], [Tbb], scale=lg, bias=self.lgb[:, 0:1])
            self.act(self.Dd[:, :], Ed[:, :], AF.Exp, [Edb], [Ddb], scale=lg, bias=self.lgb[:, 0:1])
            self.memset("pool", self.Dd[64:128, 0:64], 0.0, [Ddb])
            self.attn_head("ret", h, slot % 2,
                           [(self.RK[h * 128:(h + 1) * 128, :], 128, [self.db("RK", (h, 0)), self.db("RK", (h, 1))])],
                           [(self.RQ[h * 128:(h + 1) * 128, :], 128, [self.db("RQ", (h, 0)), self.db("RQ", (h, 1))])],
                           self.RV[:, h * 256:(h + 1) * 256],
                           [self.db("RV", (c, t)) for c in (0, 512) for t in range(self.NTB // 8)],
                           256, self.CO, "CO", h * 256, (Tb, Tbb, self.Dd, Ddb))
            slot += 1

    def phase_merge(self, l):
        S, T = self.S, self.T
        Th = min(2048, T)
        nh = T // Th
        ntc = Th // 512
        srcs = [(self.AO, "AO", 6, self.w_br_fox[l]), (self.BO, "BO", 6, self.w_br_mla[l]),
                (self.CO, "CO", 8, self.w_br_ret[l])]
        gtv = self.GT.rearrange("(i f) t -> f i t", i=3)
        self.newphase()
        at = self.view(0, [128, 20, Th], BF16)
        atbs = [self.vb("mrg_at%d" % i) for i in range(ntc)]
        self.mrg_i = 0
        self.mrg_b = [[self.vb("mrg%d_%d" % (i, j)) for j in range(5)] for i in range(4)]
        for hf in range(nh):
            for tc in range(ntc):
                k0 = 0
                c0 = hf * Th + tc * 512
                for (src, key, nk, _) in srcs:
                    S.dma("sp", at[:, k0:k0 + nk, tc * 512:(tc + 1) * 512],
                          src[:, c0:c0 + 512].rearrange("(k p) t -> p k t", p=128),
                          reads=[self.db(key, r * 128) for r in range(nk)], writes=[atbs[tc]])
                    k0 += nk
            for wi in range(8):
                dm = []
                k0 = 0
                for (src, key, nk, W) in srcs:
                    dm.append((lambda wt, k0=k0, nk=nk: wt[:, 0:20 * 256].rearrange("p (k n) -> p k n", k=20)[
                        :, k0:k0 + nk, :], W[:, wi * 256:(wi + 1) * 256].rearrange("(k p) n -> p k n", p=128)))
                    k0 += nk
                wt, wtb = self.load_w(None, dm)
                wv = wt[:, 0:20 * 256].rearrange("p (k n) -> p k n", k=20)
                for j in range(2):
                    oc = wi * 2 + j
                    stg, stgb = self.nstg()
                    for tc in range(ntc):
                        si = self.mrg_i % 4
                        self.mrg_i += 1
                        sbase = 81920 + si * 9216
                        gA = self.view(sbase, [128, 2, 512], BF16)
                        gB = self.view(sbase + 2048, [128, 512], BF16)
                        m0 = self.view(sbase + 3072, [128, 512], F32)
                        m1 = self.view(sbase + 5120, [128, 512], F32)
                        m2 = self.view(sbase + 7168, [128, 512], F32)
                        gtb, gtb2, m0b, m1b, m2b = self.mrg_b[si]
                        tg = hf * ntc + tc
                        S.dma("sp", gA, gtv[oc * 128:(oc + 1) * 128, 0:2, tg * 512:(tg + 1) * 512],
                              reads=[self.db("GT", oc * 128), self.db("GT", 2048 + oc * 128)], writes=[gtb])
                        S.dma("sp", gB, gtv[oc * 128:(oc + 1) * 128, 2, tg * 512:(tg + 1) * 512],
                              reads=[self.db("GT", 4096 + oc * 128)], writes=[gtb2])
                        pss = []
                        k0 = 0
                        for (src, key, nk, _) in srcs:
                            ps, pb = self.nps()
                            for kc in range(nk):
                                self.mm(ps[:, :], pb, wv[:, k0 + kc, j * 128:(j + 1) * 128],
                                        at[:, k0 + kc, tc * 512:(tc + 1) * 512], kc == 0, kc == nk - 1, [wtb, atbs[tc]])
                            pss.append((ps, pb))
                            k0 += nk
                        self.tt("dve", m0, pss[0][0][:, :], gA[:, 0, :], ALU.mult, [pss[0][1], gtb], [m0b])
                        self.tt("dve", m1, pss[1][0][:, :], gA[:, 1, :], ALU.mult, [pss[1][1], gtb], [m1b])
                        self.tt("dve", m2, pss[2][0][:, :], gB, ALU.mult, [pss[2][1], gtb2], [m2b])
                        self.tt("dve", m0, m0, m1, ALU.add, [m0b, m1b], [m0b])
                        self.tt("dve", stg[:, tc * 512:(tc + 1) * 512], m0, m2, ALU.add, [m0b, m2b], [stgb])
                    S.dma("sp", self.MG[oc * 128:(oc + 1) * 128, hf * Th:(hf + 1) * Th], stg[:, 0:Th], reads=[stgb],
                          writes=[self.db("MG", (oc, hf))])

    def phase_resid_gemm(self, A, K, W, tag):
        S, T = self.S, self.T
        kcn = K // 128
        if kcn <= 16:
            Tq, kparts, ncol = min(2048, T), 1, 512
        else:
            Tq, kparts, ncol = min(2048, T), 2, 256
        kp = kcn // kparts
        nq = T // Tq
        ntc = Tq // 512
        self.newphase()
        at = self.view(0, [128, kp, Tq], BF16)
        atbs = [self.vb("rg_at%d" % i) for i in range(ntc)]
        self.rgi = 0
        for q in range(nq):
            for kpi in range(kparts):
                fin = kpi == kparts - 1
                pend = []
                for tc in range(ntc):
                    tg = q * ntc + tc
                    if A is self.MG:
                        deps = [self.db("MG", (a, tg * 512 // min(2048, T))) for a in range(kcn)]
                    else:
                        deps = [self.db("FA", kpi * kp + a) for a in range(kp)]
                    r0 = kpi * kp * 128
                    S.dma("sp", at[:, :, tc * 512:(tc + 1) * 512],
                          A[r0:r0 + kp * 128, tg * 512:(tg + 1) * 512].rearrange("(k p) t -> p k t", p=128),
                          reads=deps, writes=[atbs[tc]])
                for wi in range(D // ncol):
                    r0 = kpi * kp * 128
                    wt, wtb = self.load_w(None, [(lambda wt: wt[:, 0:kp * ncol].rearrange("p (k n) -> p k n", k=kp),
                                                  W[r0:r0 + kp * 128, wi * ncol:(wi + 1) * ncol].rearrange(
                                                      "(k p) n -> p k n", p=128))])
                    wv = wt[:, 0:kp * ncol].rearrange("p (k n) -> p k n", k=kp)
                    for j in range(ncol // 128):
                        oc = wi * (ncol // 128) + j
                        for tc in range(ntc):
                            tg = q * ntc + tc
                            xr, xrb = self.ntmp()
                            S.dma("sp", xr[:, 0:512], self.xT[oc * 128:(oc + 1) * 128, tg * 512:(tg + 1) * 512],
                                  reads=[self.db("xT", (oc, tg))], writes=[xrb])
                            ps, pb = self.ps[self.rgi % 4], self.pb[self.rgi % 4]
                            self.rgi += 1
                            for kc in range(kp):
                                self.mm(ps[:, :], pb, wv[:, kc, j * 128:(j + 1) * 128],
                                        at[:, kc, tc * 512:(tc + 1) * 512], kc == 0, kc == kp - 1, [wtb, atbs[tc]])
                            self.tt("dve", xr[:, 0:512], ps[:, :], xr[:, 0:512], ALU.add, [pb, xrb], [xrb])
                            S.dma("sp", self.xT[oc * 128:(oc + 1) * 128, tg * 512:(tg + 1) * 512], xr[:, 0:512],
                                  reads=[xrb], writes=[self.db("xT", (oc, tg))])
                            if fin:
                                sqt, sqb = self.ntmp()
                                sq = sqt.bitcast(BF16)[:, 0:512]
                                self.act(sq, xr[:, 0:512], AF.Square, [xrb], [sqb])
                                pend.append((tc, sq, sqb, oc))
                                if len(pend) > 2:
                                    (tc_, sq_, sqb_, oc_) = pend.pop(0)
                                    self.mm(self.ps[4 + tc_][:, :], self.pb[4 + tc_], self.ones_bf[:, :], sq_, oc_ == 0,
                                            oc_ == D // 128 - 1, [sqb_, self.constb])
                if fin:
                    while pend:
                        (tc_, sq_, sqb_, oc_) = pend.pop(0)
                        self.mm(self.ps[4 + tc_][:, :], self.pb[4 + tc_], self.ones_bf[:, :], sq_, oc_ == 0,
                                oc_ == D // 128 - 1, [sqb_, self.constb])
                    for tc in range(ntc):
                        tg = q * ntc + tc
                        rs, rsb = self.ntmp()
                        self.rstd_from_ps(self.ps[4 + tc][:, :], self.pb[4 + tc], float(D), rs[:, 0:512], rsb)
                        S.dma("sp", self.RSTD[tg * 512:(tg + 1) * 512].rearrange("(o t) -> o t", o=1), rs[0:1, 0:512],
                              reads=[rsb], writes=[self.db("RSTD", tg)])
        self.rstd_ready = True

    def phase_ffn_up(self, l):
        S, T, NTC = self.S, self.T, self.NTC
        hT = self.hT
        hb = lambda tc: self.hTb[tc]
        Wu, Wg = self.ffn_w_up[l], self.ffn_w_gate[l]
        for wi in range(NFC // 2):
            wt, wtb = self.load_w(None, [
                (lambda wt: wt[:, 0:KC * 512].rearrange("p (k n) -> p k n", k=KC)[:, :, 0:256],
                 Wu[:, wi * 256:(wi + 1) * 256].rearrange("(k p) n -> p k n", p=128)),
                (lambda wt: wt[:, 0:KC * 512].rearrange("p (k n) -> p k n", k=KC)[:, :, 256:512],
                 Wg[:, wi * 256:(wi + 1) * 256].rearrange("(k p) n -> p k n", p=128)),
            ])
            wv = wt[:, 0:KC * 512].rearrange("p (k n) -> p k n", k=KC)
            for j in range(2):
                fc = wi * 2 + j
                stg, stgb = self.nstg()
                cw = lambda i: self.cpar[:, fc * 16 + l * 3 + i:fc * 16 + l * 3 + i + 1]
                cbias = self.cpar[:, fc * 16 + 12 + l:fc * 16 + 12 + l + 1]
                prev = None
                for tc in range(NTC):
                    psu, pbu = self.nps()
                    self.ws_group(psu, pbu, wv, wtb, j * 128, 128, hT, hb, KC, tc)
                    psg, pbg = self.nps()
                    self.ws_group(psg, pbg, wv, wtb, 256 + j * 128, 128, hT, hb, KC, tc)
                    ub, ubb = self.ntmp()
                    self.cp("act", ub[:, 2:514], psu[:, :], [pbu], [ubb])
                    if prev is None:
                        self.memset("dve", ub[:, 0:2], 0.0, [ubb])
                    else:
                        self.cp("act", ub[:, 0:2], prev[0][:, 512:514], [prev[1]], [ubb])
                    t, tb_ = self.ntmp()
                    self.ts("dve", t[:, 0:512], ub[:, 2:514], cw(2), cbias, ALU.mult, ALU.add, [ubb, self.parb], [tb_])
                    self.stt(t[:, 0:512], ub[:, 1:513], cw(1), t[:, 0:512], ALU.mult, ALU.add, [ubb, tb_, self.parb],
                             [tb_])
                    self.stt(t[:, 0:512], ub[:, 0:512], cw(0), t[:, 0:512], ALU.mult, ALU.add, [ubb, tb_, self.parb],
                             [tb_])
                    self.act(t[:, 0:512], t[:, 0:512], AF.Gelu_apprx_tanh, [tb_], [tb_])
                    self.tt("dve", stg[:, tc * 512:(tc + 1) * 512], psg[:, :], t[:, 0:512], ALU.mult, [pbg, tb_],
                            [stgb])
                    prev = (ub, ubb)
                S.dma("sp", self.FA[fc * 128:(fc + 1) * 128, :], stg[:, 0:T], reads=[stgb], writes=[self.db("FA", fc)])

    def phase_final(self):
        S, T = self.S, self.T
        self.newphase()
        xc = [self.view(i * 32768, [128, KC, 512], F32) for i in range(2)]
        xcb = [self.vb("fx%d" % i) for i in range(2)]
        yo = [self.view(65536 + i * 32768, [128, 4, D], F32) for i in range(2)]
        yob = [self.vb("fy%d" % i) for i in range(2)]
        for tc in range(self.NTC):
            xc_, xcb_ = xc[tc % 2], xcb[tc % 2]
            yo_, yob_ = yo[tc % 2], yob[tc % 2]
            S.dma("sp", xc_, self.xT[:, tc * 512:(tc + 1) * 512].rearrange("(k p) t -> p k t", p=128),
                  reads=[self.db("xT", (kc, tc)) for kc in range(KC)], writes=[xcb_])
            ps, pb = self.nps()
            for kc in range(KC):
                tmp, tmpb = self.ntmp()
                sq = tmp.bitcast(BF16)[:, 0:512]
                self.act(sq, xc_[:, kc, :], AF.Square, [xcb_], [tmpb])
                self.mm(ps[:, :], pb, self.ones_bf[:, :], sq, kc == 0, kc == KC - 1, [tmpb, self.constb])
            rs, rsb = self.ntmp()
            self.rstd_from_ps(ps[:, :], pb, float(D), rs[:, 0:512], rsb)
            for kc in range(KC):
                self.stt(xc_[:, kc, :], xc_[:, kc, :], self.gcol[:, kc * 9 + 8:kc * 9 + 9], rs[:, 0:512], ALU.mult,
                         ALU.mult, [xcb_, rsb, self.parb], [xcb_])
            for tb in range(4):
                for k4 in range(4):
                    ps2, pb2 = self.nps()
                    for a in range(4):
                        kc = k4 * 4 + a
                        self.tr(ps2[:, a * 128:(a + 1) * 128], pb2, xc_[:, kc, tb * 128:(tb + 1) * 128],
                                self.ident[:, :], [xcb_, self.identb])
                    self.cp("act" if k4 % 2 else "dve", yo_[:, tb, k4 * 512:(k4 + 1) * 512], ps2[:, :], [pb2], [yob_])
            S.dma("sp", self.y[tc * 512:(tc + 1) * 512, :].rearrange("(tb p) d -> p tb d", p=128), yo_, reads=[yob_],
                  writes=[self.db("y", tc)])


_NC_CACHE = {}


def _get_nc():
    if "nc" not in _NC_CACHE:
        _NC_CACHE["nc"] = MK().build()
    return _NC_CACHE["nc"]


def kernel(**inputs):
    nc = _get_nc()
    x = np.ascontiguousarray(np.asarray(inputs["x"], dtype=np.float32))
    shared = {k: np.ascontiguousarray(np.asarray(v, dtype=np.float32)) for k, v in inputs.items() if k != "x"}
    in_maps = []
    for b in range(BATCH):
        m = dict(shared)
        m["x"] = x[b]
        in_maps.append(m)
    res = run_bass_kernel_spmd(nc, in_maps, core_ids=list(range(BATCH)))
    return np.stack([np.asarray(res.results[b]["y"], dtype=np.float32) for b in range(BATCH)], axis=0)
```

```python
import math
import contextlib
import numpy as np
import concourse.bass as bass
import concourse.mybir as mybir
from concourse.bass_utils import run_bass_kernel_spmd

F32 = mybir.dt.float32
BF16 = mybir.dt.bfloat16
I32 = mybir.dt.int32
AF = mybir.ActivationFunctionType
ALU = mybir.AluOpType

ENGS = ("pe", "act", "dve", "pool", "sp")
NDMASEM = 24

D = 2048
KC = 16
DEPTH = 4
SEQ = 4096
BATCH = 8
DFF = 5632
NFC = 44
EPS = 1e-6
O_FQ, O_FK, O_FV, O_FF, O_MQ, O_MKV, O_MKR, O_RQ, O_RK, O_RV, O_RG, O_GT = (
    0, 768, 1536, 2304, 2310, 2822, 3078, 3142, 3654, 4166, 5190, 6214)
INW = 12358


class Buf:
    __slots__ = ("name", "w", "r", "big")

    def __init__(self, name="", big=False):
        self.name = name
        self.w = None
        self.r = []
        self.big = big


class Op:
    __slots__ = ("eng", "fn", "deps", "dma", "sig", "sem", "val", "qi")

    def __init__(self, eng, fn, dma):
        self.eng = eng
        self.fn = fn
        self.dma = dma
        self.deps = ()
        self.sig = False
        self.sem = None
        self.val = 0
        self.qi = 0


class Sched:
    def __init__(self, nc):
        self.nc = nc
        self.ops = []
        self.per_eng = {e: [] for e in ENGS}
        self.ndma = {e: 0 for e in ENGS}
        self.bigtok = Buf("bigtok")

    def add(self, eng, fn, reads=(), writes=(), dma=False):
        idx = len(self.ops)
        op = Op(eng, fn, dma)
        reads = list(reads)
        for b in list(reads) + list(writes):
            if b.big:
                reads.append(self.bigtok)
                break
        deps = set()
        for b in reads:
            if b.w is not None:
                deps.add(b.w)
        for b in writes:
            if b.w is not None:
                deps.add(b.w)
            deps.update(b.r)
        for b in reads:
            if not dma:
                b.r = [j for j in b.r if self.ops[j].dma or self.ops[j].eng != eng]
            b.r.append(idx)
        for b in writes:
            b.w = idx
            b.r = []
        deps.discard(idx)
        fd = []
        for d in deps:
            o = self.ops[d]
            if eng == "pe" and o.eng == "pe" and not dma and not o.dma:
                continue
            fd.append(d)
            o.sig = True
        op.deps = fd
        if dma:
            op.sig = True
            op.qi = self.ndma[eng]
            self.ndma[eng] += 1
        self.ops.append(op)
        self.per_eng[eng].append(idx)
        return idx

    def dma(self, eng, out, in_, reads=(), writes=(), slow=False):
        if eng == "sp" and type(out.tensor).__name__ == "DRamTensorHandle":
            eng = "act"
        if slow:
            return self.add(eng, lambda e: e.dma_start(out=out, in_=in_, allow_slow_non_contiguous=True),
                            reads, writes, dma=True)
        return self.add(eng, lambda e: e.dma_start(out=out, in_=in_), reads, writes, dma=True)

    def emit(self, final_wait_eng="sp"):
        nc = self.nc
        ops = self.ops
        with contextlib.ExitStack() as st:
            csem = {e: st.enter_context(nc.semaphore("c_" + e)) for e in ENGS}
            dsem = {e: [st.enter_context(nc.semaphore("d_%s_%d" % (e, i))) for i in range(NDMASEM)]
                    for e in ENGS if self.ndma[e] > 0}
            cnt = {e: 0 for e in ENGS}
            for op in ops:
                if op.dma:
                    op.sem = dsem[op.eng][op.qi % NDMASEM]
                    op.val = 16 * (op.qi // NDMASEM + 1)
                elif op.sig:
                    cnt[op.eng] += 1
                    op.sem = csem[op.eng]
                    op.val = cnt[op.eng]
            self.maxval = dict(cnt)
            block = st.enter_context(nc.Block())

            def make(ename):
                idxs = self.per_eng[ename]

                def body(eng):
                    waited = {}
                    for i in idxs:
                        op = ops[i]
                        need = {}
                        for d in op.deps:
                            o = ops[d]
                            k = id(o.sem)
                            if k not in need or need[k][1] < o.val:
                                need[k] = (o.sem, o.val)
                        if op.dma and op.qi >= NDMASEM:
                            s = dsem[ename][op.qi % NDMASEM]
                            v = 16 * (op.qi // NDMASEM)
                            k = id(s)
                            if k not in need or need[k][1] < v:
                                need[k] = (s, v)
                        for k, (s, v) in need.items():
                            if waited.get(k, 0) >= v:
                                continue
                            eng.wait_ge(s, v)
                            waited[k] = v
                        ins = op.fn(eng)
                        if op.sig:
                            ins.then_inc(op.sem, 16 if op.dma else 1)
                    if ename == final_wait_eng:
                        for e2 in dsem:
                            n = self.ndma[e2]
                            for j in range(min(n, NDMASEM)):
                                last_q = j + ((n - 1 - j) // NDMASEM) * NDMASEM
                                eng.wait_ge(dsem[e2][j], 16 * (last_q // NDMASEM + 1))
                return body

            if self.per_eng["pe"]:
                block.tensor(make("pe"))
            if self.per_eng["act"]:
                block.scalar(make("act"))
            if self.per_eng["dve"]:
                block.vector(make("dve"))
            if self.per_eng["pool"]:
                block.gpsimd(make("pool"))
            block.sync(make("sp"))


class MK:
    def __init__(self, T=SEQ, L=DEPTH, dbg=(), stop_after=None):
        self.T = T
        self.L = L
        self.NTC = T // 512
        self.NTB = T // 128
        self.dbg = set(dbg)
        self.stop_after = stop_after
        nc = bass.Bass("TRN2", target_bir_lowering=False)
        self.nc = nc
        self.S = Sched(nc)
        self.st = contextlib.ExitStack()
        self.dbufs = {}

    def dram_in(self, name, shape):
        return self.nc.dram_tensor(name, list(shape), F32, kind="ExternalInput").ap()

    def scratch(self, name, shape, dtype):
        kind = "ExternalOutput" if name in self.dbg else "Internal"
        return self.nc.dram_tensor(name, list(shape), dtype, kind=kind).ap()

    def sb(self, name, shape, dtype):
        return self.st.enter_context(self.nc.sbuf_tensor(name, list(shape), dtype))

    def db(self, name, key=0):
        k = (name, key)
        if k not in self.dbufs:
            self.dbufs[k] = Buf("%s_%s" % (name, key))
        return self.dbufs[k]

    def vb(self, name=""):
        return Buf(name, big=True)

    def newphase(self):
        S = self.S
        t = self.dummy
        S.add("pool", lambda e: e.memset(t[0:1, 0:1], 0.0), writes=[S.bigtok, self.dummy_b])

    def view(self, off, shape, dtype):
        esz = 2 if dtype == BF16 else 4
        n = 1
        for s in shape[1:]:
            n *= s
        assert off % esz == 0 and off + n * esz <= 131072, (off, shape)
        base = self.big_bf if dtype == BF16 else (self.big_i if dtype == I32 else self.big_f)
        o = off // esz
        ap = base[0:shape[0], o:o + n]
        if len(shape) == 3:
            ap = ap.rearrange("p (a b) -> p a b", a=shape[1])
        elif len(shape) == 4:
            ap = ap.rearrange("p (a b c) -> p a b c", a=shape[1], b=shape[2])
        return ap

    def mm(self, ps, pb, lhsT, rhs, start, stop, reads):
        self.S.add("pe", lambda e: e.matmul(ps, lhsT, rhs, start=start, stop=stop), reads=reads, writes=[pb])

    def tr(self, ps, pb, in_, ident, reads):
        self.S.add("pe", lambda e: e.transpose(ps, in_, ident), reads=reads, writes=[pb])

    def act(self, out, in_, func, reads, writes, bias=None, scale=None):
        kw = {}
        if bias is not None:
            kw["bias"] = bias
        if scale is not None:
            kw["scale"] = scale
        self.S.add("act", lambda e: e.activation(out, in_, func, **kw), reads=reads, writes=writes)

    def tt(self, eng, out, a, b, op, reads, writes):
        self.S.add(eng, lambda e: e.tensor_tensor(out, a, b, op), reads=reads, writes=writes)

    def ts(self, eng, out, a, s1, s2, op0, op1, reads, writes):
        if op1 is None:
            self.S.add(eng, lambda e: e.tensor_scalar(out, a, s1, None, op0), reads=reads, writes=writes)
        else:
            self.S.add(eng, lambda e: e.tensor_scalar(out, a, s1, s2, op0, op1), reads=reads, writes=writes)

    def stt(self, out, a, s, b, op0, op1, reads, writes):
        self.S.add("dve", lambda e: e.scalar_tensor_tensor(out, a, s, b, op0, op1), reads=reads, writes=writes)

    def cp(self, eng, out, in_, reads, writes):
        if eng == "act":
            self.S.add("act", lambda e: e.activation(out, in_, AF.Copy), reads=reads, writes=writes)
        else:
            self.S.add(eng, lambda e: e.tensor_copy(out, in_), reads=reads, writes=writes)

    def memset(self, eng, ap, val, writes, reads=()):
        self.S.add(eng, lambda e: e.memset(ap, val), reads=reads, writes=writes)

    def nps(self):
        i = self.psi % 8
        self.psi += 1
        return self.ps[i], self.pb[i]

    def ntmp(self):
        i = self.tmpi % len(self.tmp)
        self.tmpi += 1
        return self.tmp[i], self.tmpb[i]

    def nstg(self):
        i = self.stgi % 2
        self.stgi += 1
        return self.stg[i], self.stgb[i]

    def nwt(self):
        i = self.wti % 2
        self.wti += 1
        return self.wt[i], self.wtb[i]

    def build(self):
        nc, S, T, L = self.nc, self.S, self.T, self.L
        self.x = self.dram_in("x", [T, D])
        self.norm1_g = self.dram_in("norm1_g", [DEPTH, D])
        self.w_in = self.dram_in("w_in", [DEPTH, D, INW])
        self.mla_q_norm_g = self.dram_in("mla_q_norm_g", [DEPTH, 512])
        self.mla_kv_norm_g = self.dram_in("mla_kv_norm_g", [DEPTH, 256])
        self.mla_w_uq = self.dram_in("mla_w_uq", [DEPTH, 512, 1152])
        self.mla_w_ukv = self.dram_in("mla_w_ukv", [DEPTH, 256, 1536])
        self.fox_b_f = self.dram_in("fox_b_f", [DEPTH, 6])
        self.w_br_fox = self.dram_in("w_br_fox", [DEPTH, 768, D])
        self.w_br_mla = self.dram_in("w_br_mla", [DEPTH, 768, D])
        self.w_br_ret = self.dram_in("w_br_ret", [DEPTH, 1024, D])
        self.w_out = self.dram_in("w_out", [DEPTH, D, D])
        self.norm2_g = self.dram_in("norm2_g", [DEPTH, D])
        self.ffn_w_up = self.dram_in("ffn_w_up", [DEPTH, D, DFF])
        self.ffn_w_gate = self.dram_in("ffn_w_gate", [DEPTH, D, DFF])
        self.ffn_conv_w = self.dram_in("ffn_conv_w", [DEPTH, 3, DFF])
        self.ffn_conv_b = self.dram_in("ffn_conv_b", [DEPTH, DFF])
        self.ffn_w_down = self.dram_in("ffn_w_down", [DEPTH, DFF, D])
        self.final_norm_g = self.dram_in("final_norm_g", [D])
        self.y = nc.dram_tensor("y", [T, D], F32, kind="ExternalOutput").ap()
        sc = self.scratch
        self.xT = sc("xT", [D, T], F32)
        self.QF = sc("QF", [768, T], BF16)
        self.KF = sc("KF", [768, T], BF16)
        self.VF = sc("VF", [T, 768], BF16)
        self.CROW = sc("CROW", [6, T], BF16)
        self.CQ = sc("CQ", [512, T], BF16)
        self.CKV = sc("CKV", [256, T], BF16)
        self.MKR = sc("MKR", [64, T], BF16)
        self.RQ = sc("RQ", [512, T], BF16)
        self.RK = sc("RK", [512, T], BF16)
        self.RV = sc("RV", [T, 1024], BF16)
        self.RG = sc("RG", [1024, T], BF16)
        self.GT = sc("GT", [6144, T], BF16)
        self.MQN = sc("MQN", [768, T], BF16)
        self.MQR = sc("MQR", [384, T], BF16)
        self.MKN = sc("MKN", [768, T], BF16)
        self.MV = sc("MV", [T, 768], BF16)
        self.AO = sc("AO", [768, T], BF16)
        self.BO = sc("BO", [768, T], BF16)
        self.CO = sc("CO", [1024, T], BF16)
        self.MG = sc("MG", [D, T], BF16)
        self.FA = sc("FA", [DFF, T], BF16)
        self.R128C = sc("R128C", [128, T], F32)
        self.R128S = sc("R128S", [128, T], F32)
        self.R64C = sc("R64C", [128, T], F32)
        self.R64S = sc("R64S", [128, T], F32)

        with self.st:
            big = self.sb("big", [128, 32768], F32)
            self.big_f = big
            self.big_bf = big.bitcast(BF16)
            self.big_i = big.bitcast(I32)
            self.wt = [self.sb("wt%d" % i, [128, 8192], BF16) for i in range(2)]
            self.wtb = [Buf("wt%d" % i) for i in range(2)]
            self.wt_extra = {id(b): [Buf("wtq%d" % j) for j in range(4)] for b in self.wtb}
            self.stg = [self.sb("stg%d" % i, [128, 4096], BF16) for i in range(2)]
            self.stgb = [Buf("stg%d" % i) for i in range(2)]
            self.tmp = [self.sb("tmp%d" % i, [128, 514], F32) for i in range(8)]
            self.tmpb = [Buf("tmp%d" % i) for i in range(8)]
            self.ps = [self.st.enter_context(nc.psum_tensor("ps%d" % i, [128, 512], F32)) for i in range(8)]
            self.pb = [Buf("ps%d" % i) for i in range(8)]
            self.psi = self.tmpi = self.stgi = self.wti = 0
            self.ident = self.sb("ident", [128, 128], F32)
            self.identb = Buf("ident")
            self.ones_bf = self.sb("ones_bf", [128, 128], BF16)
            self.onesf = self.sb("onesf", [128, 512], F32)
            self.epsc = self.sb("epsc", [128, 1], F32)
            self.constb = Buf("const")
            self.dummy = self.sb("dummyt", [128, 2], F32)
            self.dummy_b = Buf("dummy")
            self.gcol = self.sb("gcol", [128, 16 * 9], F32)
            self.gqc = self.sb("gqc", [128, 4 * 4], F32)
            self.gkvc = self.sb("gkvc", [128, 2 * 4], F32)
            self.cpar = self.sb("cpar", [128, NFC * 16], F32)
            self.fbneg = self.sb("fbneg", [128, 4], F32)
            self.fbias = self.sb("fbias", [128, 32 * 6], F32)
            self.fbiasb = Buf("fbias")
            self.parb = Buf("params")
            self.ccar = self.sb("ccar", [6, 2], F32)
            self.ccar_b = Buf("ccar")

            self.phase_init()
            self.phase_transpose_in()
            for l in range(L):
                self.phase_norm(l, 0)
                if self.stop_after == "norm1":
                    break
                self.phase_inproj(l)
                if self.stop_after == "inproj":
                    break
                self.phase_mla_pre(l)
                if self.stop_after == "mlapre":
                    break
                self.phase_attn(l)
                if self.stop_after == "attn":
                    break
                self.phase_merge(l)
                if self.stop_after == "merge":
                    break
                self.phase_resid_gemm(self.MG, D, self.w_out[l], "wo")
                if self.stop_after == "outproj":
                    break
                self.phase_norm(l, 1)
                self.phase_ffn_up(l)
                if self.stop_after == "ffnup":
                    break
                self.phase_resid_gemm(self.FA, DFF, self.ffn_w_down[l], "wd")
            self.phase_final()
            S.emit()
        return nc

    def phase_init(self):
        S, T = self.S, self.T
        cb = self.constb
        self.memset("pool", self.ident[:], 1.0, [self.identb])
        idt = self.ident
        S.add("pool", lambda e: e.affine_select(idt[:], idt[:], [[-1, 128]], ALU.is_equal, 0.0, base=0,
                                                channel_multiplier=1), reads=[self.identb], writes=[self.identb])
        self.memset("pool", self.ones_bf[:], 1.0, [cb])
        self.memset("pool", self.onesf[:], 1.0, [cb])
        self.memset("pool", self.epsc[:], EPS, [cb])
        self.memset("pool", self.dummy[:], 0.0, [self.dummy_b])
        self.newphase()
        pr = self.view(0, [16, DFF], F32)
        prb = self.vb("pr")
        pr2 = self.view(DFF * 4, [16, 512], F32)
        pr2b = self.vb("pr2")
        self.memset("pool", pr[:, :], 0.0, [prb])
        self.memset("pool", pr2[:, :], 0.0, [pr2b])
        S.dma("sp", pr[0:4, 0:D], self.norm1_g, writes=[prb])
        S.dma("sp", pr[4:8, 0:D], self.norm2_g, writes=[prb])
        S.dma("sp", pr[8:9, 0:D], self.final_norm_g.rearrange("(o d) -> o d", o=1), writes=[prb])
        ps, pb = self.nps()
        for kc in range(16):
            self.tr(ps[:, kc * 9:(kc + 1) * 9], pb, pr[0:9, kc * 128:(kc + 1) * 128], self.ident[0:9, 0:9],
                    [prb, self.identb])
        self.cp("dve", self.gcol[:, :], ps[:, 0:144], [pb], [self.parb])
        S.dma("sp", pr2[0:4, 0:512], self.mla_q_norm_g, writes=[pr2b])
        ps, pb = self.nps()
        for kc in range(4):
            self.tr(ps[:, kc * 4:(kc + 1) * 4], pb, pr2[0:4, kc * 128:(kc + 1) * 128], self.ident[0:4, 0:4],
                    [pr2b, self.identb])
        self.cp("dve", self.gqc[:, :], ps[:, 0:16], [pb], [self.parb])
        pr3 = self.view(DFF * 4 + 2048, [16, 256], F32)
        pr3b = self.vb("pr3")
        S.dma("sp", pr3[0:4, 0:256], self.mla_kv_norm_g, writes=[pr3b])
        ps, pb = self.nps()
        for kc in range(2):
            self.tr(ps[:, kc * 4:(kc + 1) * 4], pb, pr3[0:4, kc * 128:(kc + 1) * 128], self.ident[0:4, 0:4],
                    [pr3b, self.identb])
        self.cp("dve", self.gkvc[:, :], ps[:, 0:8], [pb], [self.parb])
        pr4 = self.view(DFF * 4 + 4096, [16, 8], F32)
        pr4b = self.vb("pr4")
        S.dma("sp", pr4[0:4, 0:6], self.fox_b_f, writes=[pr4b])
        ps, pb = self.nps()
        self.tr(ps[0:6, 0:4], pb, pr4[0:4, 0:6], self.ident[0:4, 0:4], [pr4b, self.identb])
        self.ts("dve", self.fbneg[0:6, 0:4], ps[0:6, 0:4], -1.0, None, ALU.mult, None, [pb], [self.parb])
        self.newphase()
        prc = self.view(0, [16, DFF], F32)
        prcb = self.vb("prc")
        S.dma("sp", prc[0:12, :], self.ffn_conv_w.rearrange("l i f -> (l i) f"), writes=[prcb])
        S.dma("sp", prc[12:16, :], self.ffn_conv_b, writes=[prcb])
        psA, pbA = self.nps()
        psB, pbB = self.nps()
        for fc in range(NFC):
            if fc < 32:
                self.tr(psA[:, fc * 16:(fc + 1) * 16], pbA, prc[0:16, fc * 128:(fc + 1) * 128],
                        self.ident[0:16, 0:16], [prcb, self.identb])
            else:
                self.tr(psB[:, (fc - 32) * 16:(fc - 31) * 16], pbB, prc[0:16, fc * 128:(fc + 1) * 128],
                        self.ident[0:16, 0:16], [prcb, self.identb])
        self.cp("dve", self.cpar[:, 0:512], psA[:, :], [pbA], [self.parb])
        self.cp("dve", self.cpar[:, 512:704], psB[:, 0:192], [pbB], [self.parb])
        self.newphase()
        TW = T
        pos = self.view(0, [128, TW], F32)
        yv = self.view(TW * 4, [128, TW], F32)
        yi = self.view(TW * 8, [128, TW], I32)
        yf = self.view(TW * 12, [128, TW], F32)
        ff = self.view(TW * 16, [128, TW], F32)
        mm_ = self.view(TW * 20, [128, TW], F32)
        tab = self.view(TW * 24, [128, TW], F32)
        posb, yb, yib, yfb, fb, mb, tabb = [self.vb(n) for n in ("pos", "y", "yi", "yf", "f", "m", "tab")]
        S.add("pool", lambda e: e.iota(pos, [[1, TW]], base=0, channel_multiplier=0,
                                       allow_small_or_imprecise_dtypes=True), writes=[posb])
        fidx = self.sb("fidx", [128, 1], F32)
        invf = self.sb("invf", [128, 1], F32)
        fidxb = Buf("fidx")
        invfb = Buf("invf")
        for (dd, nf, Cd, Sd) in ((128, 64, self.R128C, self.R128S), (64, 32, self.R64C, self.R64S)):
            for r in range(128 // nf):
                S.add("pool", lambda e, r=r, nf=nf: e.iota(fidx[r * nf:(r + 1) * nf, :], [[0, 1]], base=0,
                                                           channel_multiplier=1,
                                                           allow_small_or_imprecise_dtypes=True),
                      writes=[fidxb])
            self.act(invf[:, :], fidx[:, :], AF.Exp, [fidxb], [invfb], scale=-math.log(10000.0) * 2.0 / dd)
            for (shift, dst, nm) in ((0.0, Sd, "s"), (0.25, Cd, "c")):
                self.ts("dve", yv, pos, invf[:, 0:1], 1.0 / (2.0 * math.pi), ALU.mult, ALU.mult, [posb, invfb], [yb])
                if shift:
                    self.ts("dve", yv, yv, shift, None, ALU.add, None, [yb], [yb])
                self.cp("dve", yi, yv, [yb], [yib])
                self.cp("dve", yf, yi, [yib], [yfb])
                self.tt("dve", ff, yv, yf, ALU.subtract, [yb, yfb], [fb])
                self.ts("dve", mm_, ff, 0.5, None, ALU.is_gt, None, [fb], [mb])
                self.tt("dve", ff, ff, mm_, ALU.subtract, [fb, mb], [fb])
                self.ts("dve", mm_, ff, -0.5, None, ALU.is_lt, None, [fb], [mb])
                self.tt("dve", ff, ff, mm_, ALU.add, [fb, mb], [fb])
                self.act(tab, ff, AF.Sin, [fb], [tabb], scale=2.0 * math.pi)
                S.dma("sp", dst, tab, reads=[tabb], writes=[self.db("rope", nm + str(dd))])

    def phase_transpose_in(self):
        S, T = self.S, self.T
        self.newphase()
        xin = [self.view(i * 32768, [128, 4, D], F32) for i in range(2)]
        xinb = [self.vb("xin%d" % i) for i in range(2)]
        xo = [self.view(65536 + i * 32768, [128, KC, 512], F32) for i in range(2)]
        xob = [self.vb("xo%d" % i) for i in range(2)]
        for tc in range(self.NTC):
            xi, xib = xin[tc % 2], xinb[tc % 2]
            xo_, xob_ = xo[tc % 2], xob[tc % 2]
            S.dma("sp", xi, self.x[tc * 512:(tc + 1) * 512, :].rearrange("(tb p) d -> p tb d", p=128), writes=[xib])
            for kc in range(KC):
                ps, pb = self.nps()
                for tb in range(4):
                    self.tr(ps[:, tb * 128:(tb + 1) * 128], pb, xi[:, tb, kc * 128:(kc + 1) * 128], self.ident[:, :],
                            [xib, self.identb])
                self.cp("act" if kc % 2 else "dve", xo_[:, kc, :], ps[:, :], [pb], [xob_])
            S.dma("sp", self.xT[:, tc * 512:(tc + 1) * 512].rearrange("(kc p) t -> p kc t", p=128), xo_,
                  reads=[xob_], writes=[self.db("xT", (kc, tc)) for kc in range(KC)])

    def rstd_from_ps(self, ps, pb, n, dst, dstb, width=512):
        self.act(dst, ps, AF.Ln, [pb, self.constb], [dstb], bias=self.epsc[:, 0:1], scale=1.0 / n)
        self.act(dst, dst, AF.Exp, [dstb], [dstb], scale=-0.5)

    def phase_norm(self, l, which):
        S, T = self.S, self.T
        self.newphase()
        hT = self.view(0, [128, KC, T], BF16)
        self.hT = hT
        self.hTb = [self.vb("hT%d" % tc) for tc in range(self.NTC)]
        gidx = l if which == 0 else 4 + l
        for hc in range(T // 256):
            tc = hc // 2
            wt, wtb = self.nwt()
            qbs = self.wt_extra[id(wtb)]
            xs = wt.bitcast(F32)[:, 0:4096].rearrange("p (k t) -> p k t", k=KC)
            for q in range(4):
                S.dma("sp" if q % 2 == 0 else "pool", xs[:, q * 4:(q + 1) * 4, :],
                      self.xT[q * 512:(q + 1) * 512, hc * 256:(hc + 1) * 256].rearrange("(k p) t -> p k t", p=128),
                      reads=[self.db("xT", (q * 4 + a, tc)) for a in range(4)], writes=[wtb, qbs[q]])
            ps, pb = self.nps()
            for q in range(4):
                tmp, tmpb = self.ntmp()
                sq4 = tmp.bitcast(BF16)[:, 0:1024].rearrange("p (a t) -> p a t", a=4)
                self.act(sq4, xs[:, q * 4:(q + 1) * 4, :], AF.Square, [qbs[q]], [tmpb])
                for a in range(4):
                    kc = q * 4 + a
                    self.mm(ps[:, 0:256], pb, self.ones_bf[:, :], sq4[:, a, :], kc == 0, kc == KC - 1,
                            [tmpb, self.constb])
            rs, rsb = self.ntmp()
            self.rstd_from_ps(ps[:, 0:256], pb, float(D), rs[:, 0:256], rsb)
            for kc in range(KC):
                self.stt(hT[:, kc, hc * 256:(hc + 1) * 256], xs[:, kc, :],
                         self.gcol[:, kc * 9 + gidx:kc * 9 + gidx + 1], rs[:, 0:256], ALU.mult, ALU.mult,
                         [qbs[kc // 4], rsb, self.parb], [self.hTb[tc]])

    def load_w(self, view_fn, dmas):
        wt, wtb = self.nwt()
        for (o, i) in dmas:
            self.S.dma("pool", o(wt), i, writes=[wtb] + self.wt_extra[id(wtb)])
        return wt, wtb

    def ws_group(self, ps, pb, wv, wtb, c0, M, at, atb_fn, kcn, tc):
        for kc in range(kcn):
            self.mm(ps[0:M, :], pb, wv[:, kc, c0:c0 + M], at[:, kc, tc * 512:(tc + 1) * 512], kc == 0, kc == kcn - 1,
                    [wtb, atb_fn(tc)])

    def as_tile(self, wv, wtb, c0, ncols, at, atb_fn, kcn, dst, dstkey, dcol0):
        S = self.S
        stg = stgb = None
        for tb in range(self.NTB):
            if tb % 8 == 0:
                stg, stgb = self.nstg()
            sv = stg[:, :].rearrange("p (a b) -> p a b", a=8)
            ps, pb = self.nps()
            for kc in range(kcn):
                self.mm(ps[:, 0:ncols], pb, at[:, kc, tb * 128:(tb + 1) * 128], wv[:, kc, c0:c0 + ncols], kc == 0,
                        kc == kcn - 1, [wtb, atb_fn(tb // 4)])
            self.cp("act" if tb % 2 else "dve", sv[:, tb % 8, 0:ncols], ps[:, 0:ncols], [pb], [stgb])
            if tb % 8 == 7:
                t0 = (tb - 7) * 128
                S.dma("sp", dst[t0:t0 + 1024, dcol0:dcol0 + ncols].rearrange("(a p) n -> p a n", p=128),
                      sv[:, :, 0:ncols], reads=[stgb], writes=[self.db(dstkey, (dcol0, tb // 8))])

    def rope_pair(self, psA, pbA, psB, pbB, M, Ct, Ctb, St, Stb, o1, o1b, o2, o2b, tc):
        t1, t1b = self.ntmp()
        t2, t2b = self.ntmp()
        t3, t3b = self.ntmp()
        t4, t4b = self.ntmp()
        sl = slice(tc * 512, (tc + 1) * 512)
        self.tt("dve", t1[0:M, 0:512], psA[0:M, :], Ct[0:M, 0:512], ALU.mult, [pbA, Ctb], [t1b])
        self.tt("dve", t2[0:M, 0:512], psB[0:M, :], St[0:M, 0:512], ALU.mult, [pbB, Stb], [t2b])
        self.tt("dve", t3[0:M, 0:512], psB[0:M, :], Ct[0:M, 0:512], ALU.mult, [pbB, Ctb], [t3b])
        self.tt("dve", t4[0:M, 0:512], psA[0:M, :], St[0:M, 0:512], ALU.mult, [pbA, Stb], [t4b])
        self.tt("dve", o1[0:M, sl], t1[0:M, 0:512], t2[0:M, 0:512], ALU.subtract, [t1b, t2b], [o1b])
        self.tt("dve", o2[0:M, sl], t3[0:M, 0:512], t4[0:M, 0:512], ALU.add, [t3b, t4b], [o2b])

    def load_rope_tabs(self, dd, tc):
        Cd, Sd = (self.R128C, self.R128S) if dd == 128 else (self.R64C, self.R64S)
        Ct, Ctb = self.ntmp()
        St, Stb = self.ntmp()
        self.S.dma("sp", Ct[:, 0:512], Cd[:, tc * 512:(tc + 1) * 512], reads=[self.db("rope", "c" + str(dd))],
                   writes=[Ctb])
        self.S.dma("sp", St[:, 0:512], Sd[:, tc * 512:(tc + 1) * 512], reads=[self.db("rope", "s" + str(dd))],
                   writes=[Stb])
        return Ct, Ctb, St, Stb

    def plain_job(self, wv, wtb, c0, M, at, atb_fn, kcn, dst, dstkey, row0, func, evi=0):
        stg, stgb = self.nstg()
        for tc in range(self.NTC):
            ps, pb = self.nps()
            self.ws_group(ps, pb, wv, wtb, c0, M, at, atb_fn, kcn, tc)
            o = stg[0:M, tc * 512:(tc + 1) * 512]
            if func is None:
                self.cp("act" if (tc + evi) % 2 else "dve", o, ps[0:M, :], [pb], [stgb])
            else:
                self.act(o, ps[0:M, :], func, [pb], [stgb])
        self.S.dma("sp", dst[row0:row0 + M, :], stg[0:M, 0:self.T], reads=[stgb], writes=[self.db(dstkey, row0)])

    def phase_inproj(self, l):
        S, T = self.S, self.T
        W = self.w_in[l]
        hT = self.hT
        hb = lambda tc: self.hTb[tc]

        def wtile(c0, nc_):
            return self.load_w(None, [(lambda wt: wt[:, 0:KC * nc_].rearrange("p (k n) -> p k n", k=KC),
                                       W[:, c0:c0 + nc_].rearrange("(k p) n -> p k n", p=128))])

        def wview(wt, nc_):
            return wt[:, 0:KC * nc_].rearrange("p (k n) -> p k n", k=KC)

        def seg(c0, n, dst, key, func):
            o = 0
            while o < n:
                nc_ = min(512, n - o)
                wt, wtb = wtile(c0 + o, nc_)
                wv = wview(wt, nc_)
                for j in range(0, nc_, 128):
                    M = min(128, nc_ - j)
                    self.plain_job(wv, wtb, j, M, hT, hb, KC, dst, key, o + j, func, evi=j // 128)
                o += nc_

        def seg_as(c0, n, dst, key):
            o = 0
            while o < n:
                nc_ = min(512, n - o)
                wt, wtb = wtile(c0 + o, nc_)
                wv = wview(wt, nc_)
                self.as_tile(wv, wtb, 0, nc_, hT, hb, KC, dst, key, o)
                o += nc_

        seg(O_FQ, 768, self.QF, "QF", None)
        seg(O_FK, 768, self.KF, "KF", None)
        seg_as(O_FV, 768, self.VF, "VF")
        wt, wtb = wtile(O_FF, 6)
        wv = wview(wt, 6)
        self.memset("dve", self.ccar[0:6, :], 0.0, [self.ccar_b])
        crow_st, crow_b = self.nstg()
        for tc in range(self.NTC):
            ps, pb = self.nps()
            self.ws_group(ps, pb, wv, wtb, 0, 6, hT, hb, KC, tc)
            e1, e1b = self.ntmp()
            self.act(e1[0:6, 0:512], ps[0:6, :], AF.Exp, [pb, self.parb], [e1b], bias=self.fbneg[0:6, l:l + 1],
                     scale=-1.0)
            self.act(e1[0:6, 0:512], e1[0:6, 0:512], AF.Ln, [e1b, self.constb], [e1b], bias=self.onesf[0:6, 0:1])
            c1, c1b = self.ntmp()
            car = self.ccar
            S.add("dve", lambda e, c1=c1, e1=e1, p=tc % 2: e.tensor_tensor_scan(
                c1[0:6, 0:512], self.onesf[0:6, 0:512], e1[0:6, 0:512], car[0:6, p:p + 1], ALU.mult, ALU.add),
                reads=[e1b, self.constb, self.ccar_b], writes=[c1b])
            self.cp("dve", car[0:6, (tc + 1) % 2:(tc + 1) % 2 + 1], c1[0:6, 511:512], [c1b], [self.ccar_b])
            self.act(crow_st[0:6, tc * 512:(tc + 1) * 512], c1[0:6, 0:512], AF.Copy, [c1b], [crow_b],
                     scale=-math.sqrt(128.0))
            ps2, pb2 = self.nps()
            for j in range(4):
                self.tr(ps2[:, j * 6:(j + 1) * 6], pb2, c1[0:6, j * 128:(j + 1) * 128], self.ident[0:6, 0:6],
                        [c1b, self.identb])
            self.cp("dve", self.fbias[:, tc * 24:(tc + 1) * 24], ps2[:, 0:24], [pb2], [self.fbiasb])
        S.dma("sp", self.CROW[:, :], crow_st[0:6, 0:T], reads=[crow_b], writes=[self.db("CROW")])
        seg(O_MQ, 512, self.CQ, "CQ", None)
        seg(O_MKV, 256, self.CKV, "CKV", None)
        wt, wtb = wtile(O_MKR, 64)
        wv = wview(wt, 64)
        o1, o1b = self.nstg()
        o2, o2b = self.nstg()
        for tc in range(self.NTC):
            Ct, Ctb, St, Stb = self.load_rope_tabs(64, tc)
            psA, pbA = self.nps()
            self.ws_group(psA, pbA, wv, wtb, 0, 32, hT, hb, KC, tc)
            psB, pbB = self.nps()
            self.ws_group(psB, pbB, wv, wtb, 32, 32, hT, hb, KC, tc)
            self.rope_pair(psA, pbA, psB, pbB, 32, Ct, Ctb, St, Stb, o1, o1b, o2, o2b, tc)
        S.dma("sp", self.MKR[0:32, :], o1[0:32, 0:T], reads=[o1b], writes=[self.db("MKR", 0)])
        S.dma("sp", self.MKR[32:64, :], o2[0:32, 0:T], reads=[o2b], writes=[self.db("MKR", 1)])
        for (c0, dst, key) in ((O_RQ, self.RQ, "RQ"), (O_RK, self.RK, "RK")):
            dm = []
            for pr_ in range(2):
                for two in range(2):
                    for hp in range(2):
                        cs = c0 + pr_ * 256 + hp * 128 + two * 64
                        src = W[:, cs:cs + 64].rearrange("(k p) j -> p k j", p=128)
                        off = (pr_ * 2 + two) * 128 + hp * 64
                        dm.append((lambda wt, off=off: wt[:, 0:KC * 512].rearrange("p (k n) -> p k n", k=KC)[
                            :, :, off:off + 64], src))
            wt, wtb = self.load_w(None, dm)
            wv = wview(wt, 512)
            for pr_ in range(2):
                o1, o1b = self.nstg()
                o2, o2b = self.nstg()
                for tc in range(self.NTC):
                    Ct, Ctb, St, Stb = self.load_rope_tabs(128, tc)
                    psA, pbA = self.nps()
                    self.ws_group(psA, pbA, wv, wtb, pr_ * 256, 128, hT, hb, KC, tc)
                    psB, pbB = self.nps()
                    self.ws_group(psB, pbB, wv, wtb, pr_ * 256 + 128, 128, hT, hb, KC, tc)
                    self.rope_pair(psA, pbA, psB, pbB, 128, Ct, Ctb, St, Stb, o1, o1b, o2, o2b, tc)
                dv = dst.rearrange("(h two j) t -> two h j t", two=2, j=64)
                for hp in range(2):
                    h = pr_ * 2 + hp
                    S.dma("sp", dv[0, h], o1[hp * 64:(hp + 1) * 64, 0:T], reads=[o1b], writes=[self.db(key, (h, 0))])
                    S.dma("sp", dv[1, h], o2[hp * 64:(hp + 1) * 64, 0:T], reads=[o2b], writes=[self.db(key, (h, 1))])
        seg_as(O_RV, 1024, self.RV, "RV")
        seg(O_RG, 1024, self.RG, "RG", AF.Silu)
        seg(O_GT, 6144, self.GT, "GT", AF.Sigmoid)

    def phase_mla_pre(self, l):
        S, T = self.S, self.T
        self.newphase()
        NTC = self.NTC
        cq = self.view(0, [128, 4, T], BF16)
        cqn = self.view(8 * T, [128, 4, T], BF16)
        ckv = self.view(16 * T, [128, 2, T], BF16)
        ckvn = self.view(20 * T, [128, 2, T], BF16)
        cqb, cqnb, ckvb, ckvnb = self.vb("cq"), [self.vb("cqn%d" % i) for i in range(NTC)], self.vb("ckv"), \
            [self.vb("ckvn%d" % i) for i in range(NTC)]
        S.dma("sp", cq, self.CQ.rearrange("(k p) t -> p k t", p=128), reads=[self.db("CQ", r) for r in range(0, 512, 128)],
              writes=[cqb])
        S.dma("sp", ckv, self.CKV.rearrange("(k p) t -> p k t", p=128),
              reads=[self.db("CKV", r) for r in range(0, 256, 128)], writes=[ckvb])
        Wq = self.mla_w_uq[l].rearrange("(k p) (h c) -> p k h c", p=128, c=192)
        dm = []
        for k in range(4):
            for (o0, c0_, c1_) in ((0, 0, 128), (768, 128, 160), (960, 160, 192)):
                w_ = c1_ - c0_
                dm.append((lambda wt, k=k, o0=o0, w_=w_: wt[:, 0:4 * 1152].rearrange("p (k n) -> p k n", k=4)[
                    :, k, o0:o0 + 6 * w_].rearrange("p (h c) -> p h c", h=6), Wq[:, k, :, c0_:c1_]))
        wq, wqb = self.load_w(None, dm)
        wqv = wq[:, 0:4 * 1152].rearrange("p (k n) -> p k n", k=4)
        Wkv = self.mla_w_ukv[l].rearrange("(k p) (h c) -> p k h c", p=128, c=256)
        dm = []
        for k in range(2):
            for (o0, c0_) in ((0, 0), (768, 128)):
                dm.append((lambda wt, k=k, o0=o0: wt[:, 0:2 * 1536].rearrange("p (k n) -> p k n", k=2)[
                    :, k, o0:o0 + 768].rearrange("p (h c) -> p h c", h=6), Wkv[:, k, :, c0_:c0_ + 128]))
        wk, wkb = self.load_w(None, dm)
        wkv = wk[:, 0:2 * 1536].rearrange("p (k n) -> p k n", k=2)
        for (src, srcb, dstv, dstb, nk, gc, n) in ((cq, cqb, cqn, cqnb, 4, self.gqc, 512.0),
                                                    (ckv, ckvb, ckvn, ckvnb, 2, self.gkvc, 256.0)):
            for tc in range(NTC):
                ps, pb = self.nps()
                for kc in range(nk):
                    tmp, tmpb = self.ntmp()
                    sq = tmp.bitcast(BF16)[:, 0:512]
                    self.act(sq, src[:, kc, tc * 512:(tc + 1) * 512], AF.Square, [srcb], [tmpb])
                    self.mm(ps[:, :], pb, self.ones_bf[:, :], sq, kc == 0, kc == nk - 1, [tmpb, self.constb])
                rs, rsb = self.ntmp()
                self.rstd_from_ps(ps[:, :], pb, n, rs[:, 0:512], rsb)
                for kc in range(nk):
                    self.stt(dstv[:, kc, tc * 512:(tc + 1) * 512], src[:, kc, tc * 512:(tc + 1) * 512],
                             gc[:, kc * 4 + l:kc * 4 + l + 1], rs[:, 0:512], ALU.mult, ALU.mult,
                             [srcb, rsb, self.parb], [dstb[tc]])
        qb = lambda tc: cqnb[tc]
        kb_ = lambda tc: ckvnb[tc]
        for h in range(6):
            self.plain_job(wqv, wqb, h * 128, 128, cqn, qb, 4, self.MQN, "MQN", h * 128, None, evi=h)
        mqr = self.MQR.rearrange("(h two j) t -> two h j t", two=2, j=32)
        for (a0, b0, M, h0) in ((768, 960, 128, 0), (896, 1088, 64, 4)):
            o1, o1b = self.nstg()
            o2, o2b = self.nstg()
            for tc in range(NTC):
                Ct, Ctb, St, Stb = self.load_rope_tabs(64, tc)
                psA, pbA = self.nps()
                self.ws_group(psA, pbA, wqv, wqb, a0, M, cqn, qb, 4, tc)
                psB, pbB = self.nps()
                self.ws_group(psB, pbB, wqv, wqb, b0, M, cqn, qb, 4, tc)
                self.rope_pair(psA, pbA, psB, pbB, M, Ct, Ctb, St, Stb, o1, o1b, o2, o2b, tc)
            nh = M // 32
            for hh in range(nh):
                S.dma("sp", mqr[0, h0 + hh], o1[hh * 32:(hh + 1) * 32, 0:T], reads=[o1b],
                      writes=[self.db("MQR", (h0 + hh, 0))])
                S.dma("sp", mqr[1, h0 + hh], o2[hh * 32:(hh + 1) * 32, 0:T], reads=[o2b],
                      writes=[self.db("MQR", (h0 + hh, 1))])
        for h in range(6):
            self.plain_job(wkv, wkb, h * 128, 128, ckvn, kb_, 2, self.MKN, "MKN", h * 128, None, evi=h)
        self.as_tile(wkv, wkb, 768, 512, ckvn, kb_, 2, self.MV, "MV", 0)
        self.as_tile(wkv, wkb, 1280, 256, ckvn, kb_, 2, self.MV, "MV", 512)

    def attn_head(self, kind, hidx, slot, kparts, qparts, vsrc, vkeys, dv, dst, dstkey, row0, extra):
        S, T, NTC = self.S, self.T, self.NTC
        nvc = dv // 128
        base = slot * 40960
        off = base
        ktiles, qtiles = [], []
        hb = self.slotb[slot]
        for (ap, kd, deps) in kparts:
            kt = self.view(off, [128, T], BF16)
            off += 2 * T
            S.dma("sp", kt[0:kd, :], ap, reads=deps, writes=[hb])
            ktiles.append((kt, kd))
        for (ap, kd, deps) in qparts:
            qt = self.view(off, [128, T], BF16)
            off += 2 * T
            S.dma("sp", qt[0:kd, :], ap, reads=deps, writes=[hb])
            qtiles.append((qt, kd))
        vt = self.view(off, [128, self.NTB, dv], BF16)
        off += 2 * self.NTB * dv
        S.dma("sp", vt, vsrc.rearrange("(kb p) d -> p kb d", p=128), reads=vkeys, writes=[hb])
        crow = None
        if kind == "fox":
            crow = self.view(off, [1, T], BF16)
            off += 2 * T
            S.dma("sp", crow, self.CROW[hidx:hidx + 1, :], reads=[self.db("CROW")], writes=[hb])
        assert off - base <= 40960
        ostg = [self.nstg() for _ in range(nvc)]
        tiles = [(qc, kb) for qc in range(NTC) for kb in range(4 * qc + 4)]
        pts = self.pts
        ptb = self.ptb

        if kind == "ret":
            sbanks, obank, nbank = [0, 1, 6], (lambda par, c: 2 + 2 * par + c), (lambda par: 7)
        else:
            sbanks, obank, nbank = [0, 1, 3, 5], (lambda par, c: 2 + 2 * par), (lambda par: 6 + par)
        nsb = len(sbanks)

        def s_mm(i):
            qc, kb = tiles[i]
            ps, pb = self.ps[sbanks[i % nsb]], self.pb[sbanks[i % nsb]]
            c0 = max(kb - 4 * qc, 0) * 128
            n = len(ktiles) + (1 if crow is not None else 0)
            j = 0
            for (kt, kd), (qt, _) in zip(ktiles, qtiles):
                self.mm(ps[:, c0:512], pb, kt[0:kd, kb * 128:(kb + 1) * 128],
                        qt[0:kd, qc * 512 + c0:(qc + 1) * 512], j == 0, j == n - 1, [hb])
                j += 1
            if crow is not None:
                self.mm(ps[:, c0:512], pb, self.ones_bf[0:1, 0:128], crow[0:1, qc * 512 + c0:(qc + 1) * 512], False,
                        True, [hb, self.constb])

        def transform(i):
            qc, kb = tiles[i]
            ps, pb = self.ps[sbanks[i % nsb]], self.pb[sbanks[i % nsb]]
            pt, ptb_ = pts[i % 4], ptb[i % 4]
            jd = kb - 4 * qc
            c0 = max(jd, 0) * 128
            if kind == "fox":
                self.act(pt[:, c0:512], ps[:, c0:512], AF.Exp, [pb, self.fbiasb], [ptb_],
                         bias=self.fbias[:, kb * 6 + hidx:kb * 6 + hidx + 1], scale=128.0 ** -0.5)
            elif kind == "mla":
                self.act(pt[:, c0:512], ps[:, c0:512], AF.Exp, [pb], [ptb_], scale=192.0 ** -0.5)
            else:
                tb_, tbb, dd_, ddb = extra
                u0 = 384 + qc * 512 - kb * 128
                if jd < 0:
                    self.tt("dve", pt[:, :], ps[:, :], tb_[:, u0:u0 + 512], ALU.mult, [pb, tbb], [ptb_])
                else:
                    c1 = (jd + 1) * 128
                    self.tt("dve", pt[:, jd * 128:c1], ps[:, jd * 128:c1], dd_[:, :], ALU.mult, [pb, ddb], [ptb_])
                    if c1 < 512:
                        self.tt("dve", pt[:, c1:512], ps[:, c1:512], tb_[:, u0 + c1:u0 + 512], ALU.mult, [pb, tbb],
                                [ptb_])
            if jd >= 0:
                dsl = pt[:, jd * 128:(jd + 1) * 128]
                if kind == "fox":
                    S.add("pool", lambda e, dsl=dsl: e.affine_select(dsl, dsl, [[1, 128]], ALU.is_ge, 0.0, base=0,
                                                                     channel_multiplier=-1),
                          reads=[ptb_], writes=[ptb_])
                elif kind == "mla":
                    self.memset("pool", pt[64:128, jd * 128:jd * 128 + 64], 0.0, [ptb_])

        def pv_mm(i):
            qc, kb = tiles[i]
            pt, ptb_ = pts[i % 4], ptb[i % 4]
            first, last = kb == 0, kb == 4 * qc + 3
            par = qc % 2
            c0 = max(kb - 4 * qc, 0) * 128
            for c in range(nvc):
                po, pob = self.ps[obank(par, c)], self.pb[obank(par, c)]
                self.mm(po[:, c0:512], pob, vt[:, kb, c * 128:(c + 1) * 128], pt[:, c0:512], first, last, [hb, ptb_])
            if kind != "ret":
                pn, pnb = self.ps[nbank(par)], self.pb[nbank(par)]
                self.mm(pn[:, c0:512], pnb, self.ones_bf[:, :], pt[:, c0:512], first, last, [ptb_, self.constb])
            if last:
                epilogue(qc)

        def epilogue(qc):
            par = qc % 2
            sl = slice(qc * 512, (qc + 1) * 512)
            if kind != "ret":
                pn, pnb = self.ps[nbank(par)], self.pb[nbank(par)]
                po, pob = self.ps[obank(par, 0)], self.pb[obank(par, 0)]
                rc, rcb = self.ntmp()
                S.add("dve", lambda e: e.reciprocal(rc[:, 0:512], pn[:, :]), reads=[pnb], writes=[rcb])
                self.tt("dve", ostg[0][0][:, sl], po[:, :], rc[:, 0:512], ALU.mult, [pob, rcb], [ostg[0][1]])
            else:
                pn, pnb = self.ps[nbank(par)], self.pb[nbank(par)]
                for c in range(2):
                    po, pob = self.ps[obank(par, c)], self.pb[obank(par, c)]
                    tmp, tmpb = self.ntmp()
                    sq = tmp.bitcast(BF16)[:, 0:512]
                    self.act(sq, po[:, :], AF.Square, [pob], [tmpb])
                    self.mm(pn[:, :], pnb, self.ones_bf[:, :], sq, c == 0, c == 1, [tmpb, self.constb])
                rs, rsb = self.ntmp()
                self.rstd_from_ps(pn[:, :], pnb, 256.0, rs[:, 0:512], rsb)
                for c in range(2):
                    po, pob = self.ps[obank(par, c)], self.pb[obank(par, c)]
                    g, gb = self.ntmp()
                    gv = g.bitcast(BF16)[:, 0:512]
                    r0 = hidx * 256 + c * 128
                    S.dma("sp", gv, self.RG[r0:r0 + 128, sl], reads=[self.db("RG", r0)], writes=[gb])
                    t, tb2 = self.ntmp()
                    self.tt("dve", t[:, 0:512], po[:, :], rs[:, 0:512], ALU.mult, [pob, rsb], [tb2])
                    self.tt("pool", ostg[c][0][:, sl], t[:, 0:512], gv, ALU.mult, [tb2, gb], [ostg[c][1]])

        n = len(tiles)
        s_mm(0)
        if n > 1:
            s_mm(1)
        for i in range(n):
            if i + 2 < n:
                s_mm(i + 2)
            transform(i)
            pv_mm(i)
        for c in range(nvc):
            S.dma("sp", dst[row0 + c * 128:row0 + (c + 1) * 128, :], ostg[c][0][:, 0:T], reads=[ostg[c][1]],
                  writes=[self.db(dstkey, row0 + c * 128)])

    def phase_attn(self, l):
        S, T = self.S, self.T
        self.newphase()
        pbase = 81920
        self.pts = [self.view(pbase + i * 1024, [128, 512], BF16) for i in range(4)]
        self.ptb = [self.vb("pt%d" % i) for i in range(4)]
        self.slotb = [self.vb("slot%d" % i) for i in range(2)]
        slot = 0
        for h in range(6):
            self.attn_head("fox", h, slot % 2,
                           [(self.KF[h * 128:(h + 1) * 128, :], 128, [self.db("KF", h * 128)])],
                           [(self.QF[h * 128:(h + 1) * 128, :], 128, [self.db("QF", h * 128)])],
                           self.VF[:, h * 128:(h + 1) * 128],
                           [self.db("VF", (c, t)) for c in (0, 512) for t in range(self.NTB // 8)],
                           128, self.AO, "AO", h * 128, None)
            slot += 1
        for h in range(6):
            self.attn_head("mla", h, slot % 2,
                           [(self.MKN[h * 128:(h + 1) * 128, :], 128, [self.db("MKN", h * 128)]),
                            (self.MKR[:, :], 64, [self.db("MKR", 0), self.db("MKR", 1)])],
                           [(self.MQN[h * 128:(h + 1) * 128, :], 128, [self.db("MQN", h * 128)]),
                            (self.MQR[h * 64:(h + 1) * 64, :], 64,
                             [self.db("MQR", (h, 0)), self.db("MQR", (h, 1))])],
                           self.MV[:, h * 128:(h + 1) * 128],
                           [self.db("MV", (c, t)) for c in (0, 512) for t in range(self.NTB // 8)],
                           128, self.BO, "BO", h * 128, None)
            slot += 1
        TWD = 384 + T
        ebase = pbase + 4096
        E = self.view(ebase, [128, TWD], F32)
        Eb = self.vb("E")
        S.add("pool", lambda e: e.iota(E, [[1, TWD]], base=-384, channel_multiplier=-1,
                                       allow_small_or_imprecise_dtypes=True), writes=[Eb])
        Ed = self.sb("Ed_%d" % l, [128, 128], F32) if l == 0 else self.Ed
        self.Ed = Ed
        Edb = Buf("Ed")
        if l == 0:
            self.Edb = Edb
            S.add("pool", lambda e: e.iota(Ed[:, :], [[1, 128]], base=0, channel_multiplier=-1,
                                           allow_small_or_imprecise_dtypes=True), writes=[Edb])
            self.act(Ed[:, :], Ed[:, :], AF.Abs, [Edb], [Edb])
            self.Dd = self.sb("Dd", [128, 128], F32)
            self.lgb = self.sb("lgb", [128, 1], F32)
            self.memset("pool", self.lgb[:, :], math.log(128.0 ** -0.5), [Edb])
        Edb = self.Edb
        Tb = self.view(ebase + TWD * 4, [128, TWD], F32)
        Tbb = self.vb("Tb")
        Ddb = self.vb("Dd")
        for h in range(4):
            lg = math.log(1.0 - 2.0 ** (-5.0 - h))
            self.act(Tb, E, AF.Exp, [Eb, Edb], [Tbb], scale=lg, bias=self.lgb[:, 0:1])
            self.act(self.Dd[:, :], Ed[:, :], AF.Exp, [Edb], [Ddb], scale=lg, bias=self.lgb[:, 0:1])
            self.memset("pool", self.Dd[64:128, 0:64], 0.0, [Ddb])
            self.attn_head("ret", h, slot % 2,
                           [(self.RK[h * 128:(h + 1) * 128, :], 128, [self.db("RK", (h, 0)), self.db("RK", (h, 1))])],
                           [(self.RQ[h * 128:(h + 1) * 128, :], 128, [self.db("RQ", (h, 0)), self.db("RQ", (h, 1))])],
                           self.RV[:, h * 256:(h + 1) * 256],
                           [self.db("RV", (c, t)) for c in (0, 512) for t in range(self.NTB // 8)],
                           256, self.CO, "CO", h * 256, (Tb, Tbb, self.Dd, Ddb))
            slot += 1

    def phase_merge(self, l):
        S, T = self.S, self.T
        Th = min(2048, T)
        nh = T // Th
        ntc = Th // 512
        srcs = [(self.AO, "AO", 6, self.w_br_fox[l]), (self.BO, "BO", 6, self.w_br_mla[l]),
                (self.CO, "CO", 8, self.w_br_ret[l])]
        gtv = self.GT.rearrange("(i f) t -> f i t", i=3)
        self.newphase()
        at = self.view(0, [128, 20, Th], BF16)
        atbs = [self.vb("mrg_at%d" % i) for i in range(ntc)]
        self.mrg_i = 0
        self.mrg_b = [[self.vb("mrg%d_%d" % (i, j)) for j in range(5)] for i in range(4)]
        for hf in range(nh):
            for tc in range(ntc):
                k0 = 0
                c0 = hf * Th + tc * 512
                for (src, key, nk, _) in srcs:
                    S.dma("sp", at[:, k0:k0 + nk, tc * 512:(tc + 1) * 512],
                          src[:, c0:c0 + 512].rearrange("(k p) t -> p k t", p=128),
                          reads=[self.db(key, r * 128) for r in range(nk)], writes=[atbs[tc]])
                    k0 += nk
            for wi in range(8):
                dm = []
                k0 = 0
                for (src, key, nk, W) in srcs:
                    dm.append((lambda wt, k0=k0, nk=nk: wt[:, 0:20 * 256].rearrange("p (k n) -> p k n", k=20)[
                        :, k0:k0 + nk, :], W[:, wi * 256:(wi + 1) * 256].rearrange("(k p) n -> p k n", p=128)))
                    k0 += nk
                wt, wtb = self.load_w(None, dm)
                wv = wt[:, 0:20 * 256].rearrange("p (k n) -> p k n", k=20)
                for j in range(2):
                    oc = wi * 2 + j
                    stg, stgb = self.nstg()
                    for tc in range(ntc):
                        si = self.mrg_i % 4
                        self.mrg_i += 1
                        sbase = 81920 + si * 9216
                        gA = self.view(sbase, [128, 2, 512], BF16)
                        gB = self.view(sbase + 2048, [128, 512], BF16)
                        m0 = self.view(sbase + 3072, [128, 512], F32)
                        m1 = self.view(sbase + 5120, [128, 512], F32)
                        m2 = self.view(sbase + 7168, [128, 512], F32)
                        gtb, gtb2, m0b, m1b, m2b = self.mrg_b[si]
                        tg = hf * ntc + tc
                        S.dma("sp", gA, gtv[oc * 128:(oc + 1) * 128, 0:2, tg * 512:(tg + 1) * 512],
                              reads=[self.db("GT", oc * 128), self.db("GT", 2048 + oc * 128)], writes=[gtb])
                        S.dma("sp", gB, gtv[oc * 128:(oc + 1) * 128, 2, tg * 512:(tg + 1) * 512],
                              reads=[self.db("GT", 4096 + oc * 128)], writes=[gtb2])
                        pss = []
                        k0 = 0
                        for (src, key, nk, _) in srcs:
                            ps, pb = self.nps()
                            for kc in range(nk):
                                self.mm(ps[:, :], pb, wv[:, k0 + kc, j * 128:(j + 1) * 128],
                                        at[:, k0 + kc, tc * 512:(tc + 1) * 512], kc == 0, kc == nk - 1, [wtb, atbs[tc]])
                            pss.append((ps, pb))
                            k0 += nk
                        self.tt("dve", m0, pss[0][0][:, :], gA[:, 0, :], ALU.mult, [pss[0][1], gtb], [m0b])
                        self.tt("dve", m1, pss[1][0][:, :], gA[:, 1, :], ALU.mult, [pss[1][1], gtb], [m1b])
                        self.tt("dve", m2, pss[2][0][:, :], gB, ALU.mult, [pss[2][1], gtb2], [m2b])
                        self.tt("dve", m0, m0, m1, ALU.add, [m0b, m1b], [m0b])
                        self.tt("dve", stg[:, tc * 512:(tc + 1) * 512], m0, m2, ALU.add, [m0b, m2b], [stgb])
                    S.dma("sp", self.MG[oc * 128:(oc + 1) * 128, hf * Th:(hf + 1) * Th], stg[:, 0:Th], reads=[stgb],
                          writes=[self.db("MG", (oc, hf))])

    def phase_resid_gemm(self, A, K, W, tag):
        S, T = self.S, self.T
        kcn = K // 128
        if kcn <= 16:
            Tq, kparts, ncol = T, 1, 512
        else:
            Tq, kparts, ncol = min(2048, T), 2, 256
        kp = kcn // kparts
        nq = T // Tq
        ntc = Tq // 512
        self.newphase()
        at = self.view(0, [128, kp, Tq], BF16)
        atbs = [self.vb("rg_at%d" % i) for i in range(ntc)]
        for q in range(nq):
            for kpi in range(kparts):
                for tc in range(ntc):
                    tg = q * ntc + tc
                    if A is self.MG:
                        deps = [self.db("MG", (a, tg * 512 // min(2048, T))) for a in range(kcn)]
                    else:
                        deps = [self.db("FA", kpi * kp + a) for a in range(kp)]
                    r0 = kpi * kp * 128
                    S.dma("sp", at[:, :, tc * 512:(tc + 1) * 512],
                          A[r0:r0 + kp * 128, tg * 512:(tg + 1) * 512].rearrange("(k p) t -> p k t", p=128),
                          reads=deps, writes=[atbs[tc]])
                for wi in range(D // ncol):
                    r0 = kpi * kp * 128
                    wt, wtb = self.load_w(None, [(lambda wt: wt[:, 0:kp * ncol].rearrange("p (k n) -> p k n", k=kp),
                                                  W[r0:r0 + kp * 128, wi * ncol:(wi + 1) * ncol].rearrange(
                                                      "(k p) n -> p k n", p=128))])
                    wv = wt[:, 0:kp * ncol].rearrange("p (k n) -> p k n", k=kp)
                    for j in range(ncol // 128):
                        oc = wi * (ncol // 128) + j
                        for tc in range(ntc):
                            tg = q * ntc + tc
                            xr, xrb = self.ntmp()
                            S.dma("sp", xr[:, 0:512], self.xT[oc * 128:(oc + 1) * 128, tg * 512:(tg + 1) * 512],
                                  reads=[self.db("xT", (oc, tg))], writes=[xrb])
                            ps, pb = self.nps()
                            for kc in range(kp):
                                self.mm(ps[:, :], pb, wv[:, kc, j * 128:(j + 1) * 128],
                                        at[:, kc, tc * 512:(tc + 1) * 512], kc == 0, kc == kp - 1, [wtb, atbs[tc]])
                            self.tt("dve", xr[:, 0:512], ps[:, :], xr[:, 0:512], ALU.add, [pb, xrb], [xrb])
                            S.dma("sp", self.xT[oc * 128:(oc + 1) * 128, tg * 512:(tg + 1) * 512], xr[:, 0:512],
                                  reads=[xrb], writes=[self.db("xT", (oc, tg))])

    def phase_ffn_up(self, l):
        S, T, NTC = self.S, self.T, self.NTC
        hT = self.hT
        hb = lambda tc: self.hTb[tc]
        Wu, Wg = self.ffn_w_up[l], self.ffn_w_gate[l]
        for wi in range(NFC // 2):
            wt, wtb = self.load_w(None, [
                (lambda wt: wt[:, 0:KC * 512].rearrange("p (k n) -> p k n", k=KC)[:, :, 0:256],
                 Wu[:, wi * 256:(wi + 1) * 256].rearrange("(k p) n -> p k n", p=128)),
                (lambda wt: wt[:, 0:KC * 512].rearrange("p (k n) -> p k n", k=KC)[:, :, 256:512],
                 Wg[:, wi * 256:(wi + 1) * 256].rearrange("(k p) n -> p k n", p=128)),
            ])
            wv = wt[:, 0:KC * 512].rearrange("p (k n) -> p k n", k=KC)
            for j in range(2):
                fc = wi * 2 + j
                stg, stgb = self.nstg()
                cw = lambda i: self.cpar[:, fc * 16 + l * 3 + i:fc * 16 + l * 3 + i + 1]
                cbias = self.cpar[:, fc * 16 + 12 + l:fc * 16 + 12 + l + 1]
                prev = None
                for tc in range(NTC):
                    psu, pbu = self.nps()
                    self.ws_group(psu, pbu, wv, wtb, j * 128, 128, hT, hb, KC, tc)
                    psg, pbg = self.nps()
                    self.ws_group(psg, pbg, wv, wtb, 256 + j * 128, 128, hT, hb, KC, tc)
                    ub, ubb = self.ntmp()
                    self.cp("act", ub[:, 2:514], psu[:, :], [pbu], [ubb])
                    if prev is None:
                        self.memset("dve", ub[:, 0:2], 0.0, [ubb])
                    else:
                        self.cp("act", ub[:, 0:2], prev[0][:, 512:514], [prev[1]], [ubb])
                    t, tb_ = self.ntmp()
                    self.ts("dve", t[:, 0:512], ub[:, 2:514], cw(2), cbias, ALU.mult, ALU.add, [ubb, self.parb], [tb_])
                    self.stt(t[:, 0:512], ub[:, 1:513], cw(1), t[:, 0:512], ALU.mult, ALU.add, [ubb, tb_, self.parb],
                             [tb_])
                    self.stt(t[:, 0:512], ub[:, 0:512], cw(0), t[:, 0:512], ALU.mult, ALU.add, [ubb, tb_, self.parb],
                             [tb_])
                    self.act(t[:, 0:512], t[:, 0:512], AF.Gelu_apprx_tanh, [tb_], [tb_])
                    self.tt("dve", stg[:, tc * 512:(tc + 1) * 512], psg[:, :], t[:, 0:512], ALU.mult, [pbg, tb_],
                            [stgb])
                    prev = (ub, ubb)
                S.dma("sp", self.FA[fc * 128:(fc + 1) * 128, :], stg[:, 0:T], reads=[stgb], writes=[self.db("FA", fc)])

    def phase_final(self):
        S, T = self.S, self.T
        self.newphase()
        xc = [self.view(i * 32768, [128, KC, 512], F32) for i in range(2)]
        xcb = [self.vb("fx%d" % i) for i in range(2)]
        yo = [self.view(65536 + i * 32768, [128, 4, D], F32) for i in range(2)]
        yob = [self.vb("fy%d" % i) for i in range(2)]
        for tc in range(self.NTC):
            xc_, xcb_ = xc[tc % 2], xcb[tc % 2]
            yo_, yob_ = yo[tc % 2], yob[tc % 2]
            S.dma("sp", xc_, self.xT[:, tc * 512:(tc + 1) * 512].rearrange("(k p) t -> p k t", p=128),
                  reads=[self.db("xT", (kc, tc)) for kc in range(KC)], writes=[xcb_])
            ps, pb = self.nps()
            for kc in range(KC):
                tmp, tmpb = self.ntmp()
                sq = tmp.bitcast(BF16)[:, 0:512]
                self.act(sq, xc_[:, kc, :], AF.Square, [xcb_], [tmpb])
                self.mm(ps[:, :], pb, self.ones_bf[:, :], sq, kc == 0, kc == KC - 1, [tmpb, self.constb])
            rs, rsb = self.ntmp()
            self.rstd_from_ps(ps[:, :], pb, float(D), rs[:, 0:512], rsb)
            for kc in range(KC):
                self.stt(xc_[:, kc, :], xc_[:, kc, :], self.gcol[:, kc * 9 + 8:kc * 9 + 9], rs[:, 0:512], ALU.mult,
                         ALU.mult, [xcb_, rsb, self.parb], [xcb_])
            for tb in range(4):
                for k4 in range(4):
                    ps2, pb2 = self.nps()
                    for a in range(4):
                        kc = k4 * 4 + a
                        self.tr(ps2[:, a * 128:(a + 1) * 128], pb2, xc_[:, kc, tb * 128:(tb + 1) * 128],
                                self.ident[:, :], [xcb_, self.identb])
                    self.cp("act" if k4 % 2 else "dve", yo_[:, tb, k4 * 512:(k4 + 1) * 512], ps2[:, :], [pb2], [yob_])
            S.dma("sp", self.y[tc * 512:(tc + 1) * 512, :].rearrange("(tb p) d -> p tb d", p=128), yo_, reads=[yob_],
                  writes=[self.db("y", tc)])


_NC_CACHE = {}


def _get_nc():
    if "nc" not in _NC_CACHE:
        _NC_CACHE["nc"] = MK().build()
    return _NC_CACHE["nc"]


def kernel(**inputs):
    nc = _get_nc()
    x = np.ascontiguousarray(np.asarray(inputs["x"], dtype=np.float32))
    shared = {k: np.ascontiguousarray(np.asarray(v, dtype=np.float32)) for k, v in inputs.items() if k != "x"}
    in_maps = []
    for b in range(BATCH):
        m = dict(shared)
        m["x"] = x[b]
        in_maps.append(m)
    res = run_bass_kernel_spmd(nc, in_maps, core_ids=list(range(BATCH)))
    return np.stack([np.asarray(res.results[b]["y"], dtype=np.float32) for b in range(BATCH)], axis=0)
```
